# Optimizing a Trainium2 kernel written in Bass

```python
import math
import jax, jax.numpy as jnp
from jax import lax
import numpy as np

D_MODEL = 1024
BATCH = 8
SEQ = 4096
DEPTH = 4

EXPAND = 2
D_INNER = EXPAND * D_MODEL
HEAD_DIM = 64
HYENA_WIDTH = D_INNER // 2
ATTN_WIDTH = D_INNER - HYENA_WIDTH
N_HEADS_B = ATTN_WIDTH // HEAD_DIM
N_HEADS_C = D_INNER // HEAD_DIM
HYENA_SHORT = 3
HYENA_EMB_DIM = 33
HYENA_FILTER_ORDER = 64
HYENA_FAST_DECAY = 0.3
HYENA_SLOW_DECAY = 1.5
HYENA_TARGET = 1e-2
DILATED_PATTERNS = ((128, 1), (512, 4), (2048, 16))
QUERY_BLOCK = 64
GRID_W = 64
NA_ROWS_MAX = 8
NA_COLS = 16
REL_BUCKETS = 32
REL_MAX_DIST = 1024
PLE_DIM = 256
N_EVEN = (DEPTH + 1) // 2
N_ODD = DEPTH // 2
RMS_EPS = 1e-6
NEG_INF = -1e30

kernel_name = 'hybrid_hyena_dilated_natten_encoder'


def _rmsnorm(x, g):
    xf = x.astype(jnp.float32)
    y = xf * lax.rsqrt(jnp.mean(xf * xf, axis=-1, keepdims=True) + RMS_EPS)
    return (y * g.astype(jnp.float32)).astype(x.dtype)


def _short_conv(u, w, b):
    up = jnp.pad(u, ((0, 0), (1, 1), (0, 0)))
    return up[:, :-2] * w[0] + up[:, 1:-1] * w[1] + up[:, 2:] * w[2] + b


def _hyena_filters(L, w1, b1, w2, b2, w3, b3, w4, freq):
    f32 = jnp.float32
    t = jnp.linspace(0.0, 1.0, L, dtype=f32)[:, None]
    bands = (HYENA_EMB_DIM - 1) // 2
    fr = jnp.linspace(1e-4, bands - 1, bands, dtype=f32)[None, :]
    wpos = (2.0 * math.pi / L) * jnp.arange(L, dtype=f32)[:, None]
    z = jnp.concatenate([t, jnp.cos(fr * wpos), -jnp.sin(fr * wpos)], axis=-1)
    fq = freq.astype(f32)
    a = jnp.sin(fq * (z @ w1.astype(f32) + b1.astype(f32)))
    a = jnp.sin(fq * (a @ w2.astype(f32) + b2.astype(f32)))
    a = jnp.sin(fq * (a @ w3.astype(f32) + b3.astype(f32)))
    h = (a @ w4.astype(f32)).reshape(L, 2, 2, HYENA_WIDTH)
    min_decay = math.log(HYENA_TARGET) / HYENA_SLOW_DECAY
    max_decay = math.log(HYENA_TARGET) / HYENA_FAST_DECAY
    deltas = jnp.abs(jnp.linspace(min_decay, max_decay, HYENA_WIDTH, dtype=f32))
    h = h * jnp.exp(-t * deltas[None, :])[:, None, None, :]
    fwd = h[:, :, 0]
    bwd = h[1:, :, 1][::-1]
    k = jnp.concatenate([fwd, jnp.zeros((1, 2, HYENA_WIDTH), f32), bwd], axis=0)
    return k / jnp.sum(jnp.abs(k), axis=0, keepdims=True)


def _fft_conv(u, k, skip):
    L = u.shape[1]
    uf = u.astype(jnp.float32)
    spec = jnp.fft.rfft(uf, n=2 * L, axis=1) * jnp.fft.rfft(k, n=2 * L, axis=0)[None]
    y = jnp.fft.irfft(spec, n=2 * L, axis=1)[:, :L]
    return (y + uf * skip.astype(jnp.float32)).astype(u.dtype)


def _hyena(proj, conv_w, conv_b, w1, b1, w2, b2, w3, b3, w4, freq, skip):
    L = proj.shape[1]
    k = _hyena_filters(L, w1, b1, w2, b2, w3, b3, w4, freq)
    v, x1, x2 = jnp.split(_short_conv(proj, conv_w, conv_b), 3, axis=-1)
    z = x1 * _fft_conv(v, k[:, 0], skip[0])
    return x2 * _fft_conv(z, k[:, 1], skip[1])


def _t5_bucket(rel):
    half_b = REL_BUCKETS // 2
    max_exact = half_b // 2
    ret = jnp.where(rel > 0, half_b, 0)
    n = jnp.abs(rel)
    nf = jnp.maximum(n, 1).astype(jnp.float32)
    large = max_exact + (jnp.log(nf / max_exact) / math.log(REL_MAX_DIST / max_exact)
                         * (half_b - max_exact)).astype(jnp.int32)
    large = jnp.minimum(large, half_b - 1)
    return ret + jnp.where(n < max_exact, n, large)


def _dilated_band(q, k, v, window, dil, rel_bias):
    B, H, L, hd = q.shape
    n = L // dil
    half = window // (2 * dil)
    def to_res(a):
        return a.reshape(B, H, n, dil, hd).transpose(0, 1, 3, 2, 4)
    qr, kr, vr = to_res(q), to_res(k), to_res(v)
    qb = math.gcd(n, QUERY_BLOCK)
    nb = n // qb
    kw = qb + 2 * half
    pad = ((0, 0), (0, 0), (0, 0), (half, half), (0, 0))
    kp, vp = jnp.pad(kr, pad), jnp.pad(vr, pad)
    idx = (jnp.arange(nb) * qb)[:, None] + jnp.arange(kw)[None, :]
    kb, vb = kp[:, :, :, idx], vp[:, :, :, idx]
    qblk = qr.reshape(B, H, dil, nb, qb, hd)
    s = jnp.einsum('bhrnqd,bhrnkd->bhrnqk', qblk, kb).astype(jnp.float32) * (hd ** -0.5)
    rel = jnp.arange(kw)[None, :] - half - jnp.arange(qb)[:, None]
    bias = rel_bias[_t5_bucket(rel * dil)].astype(jnp.float32)
    s = s + bias.transpose(2, 0, 1)[None, :, None, None]
    keypos = idx - half
    valid = (jnp.abs(rel) <= half)[None] & ((keypos >= 0) & (keypos < n))[:, None, :]
    s = jnp.where(valid, s, NEG_INF)
    m = jnp.max(s, axis=-1, keepdims=True)
    pe = jnp.exp(s - m)
    den = jnp.sum(pe, axis=-1, keepdims=True)
    o = jnp.einsum('bhrnqk,bhrnkd->bhrnqd', (pe / den).astype(v.dtype), vb)
    lse = (m + jnp.log(den))[..., 0]
    o = o.reshape(B, H, dil, n, hd).transpose(0, 1, 3, 2, 4).reshape(B, H, L, hd)
    lse = lse.reshape(B, H, dil, n).transpose(0, 1, 3, 2).reshape(B, H, L)
    return o, lse


def _dilated_attention(qkv, rel_bias):
    B, L, _ = qkv.shape
    q, k, v = [a.reshape(B, L, N_HEADS_B, HEAD_DIM).transpose(0, 2, 1, 3)
               for a in jnp.split(qkv, 3, axis=-1)]
    outs, lses = [], []
    for window, dil in DILATED_PATTERNS:
        o, lse = _dilated_band(q, k, v, window, dil, rel_bias)
        outs.append(o.astype(jnp.float32))
        lses.append(lse)
    wts = jax.nn.softmax(jnp.stack(lses, axis=0), axis=0)
    o = jnp.einsum('pbhl,pbhld->bhld', wts, jnp.stack(outs, axis=0)).astype(qkv.dtype)
    return o.transpose(0, 2, 1, 3).reshape(B, L, ATTN_WIDTH)


def _neighbourhood_attention(qkv, rpb):
    B, L, _ = qkv.shape
    rows = L // GRID_W
    kr_n = min(NA_ROWS_MAX, rows)
    q, k, v = [a.reshape(B, rows, GRID_W, N_HEADS_C, HEAD_DIM).transpose(0, 3, 1, 2, 4)
               for a in jnp.split(qkv, 3, axis=-1)]
    cols = jnp.arange(GRID_W)
    cs = jnp.clip(cols - NA_COLS // 2, 0, GRID_W - NA_COLS)
    col_idx = cs[:, None] + jnp.arange(NA_COLS)[None, :]
    col_bias_idx = col_idx - cols[:, None] + NA_COLS - 1
    scale = HEAD_DIM ** -0.5
    def row_fn(r):
        rs = jnp.clip(r - kr_n // 2, 0, rows - kr_n)
        qr = lax.dynamic_index_in_dim(q, r, axis=2, keepdims=False)
        kr = lax.dynamic_slice_in_dim(k, rs, kr_n, axis=2)[:, :, :, col_idx]
        vr = lax.dynamic_slice_in_dim(v, rs, kr_n, axis=2)[:, :, :, col_idx]
        s = jnp.einsum('bhqd,bhrqcd->bhqrc', qr, kr).astype(jnp.float32) * scale
        row_bias_idx = rs + jnp.arange(kr_n) - r + NA_ROWS_MAX - 1
        bias = rpb[:, row_bias_idx[None, :, None], col_bias_idx[:, None, :]]
        s = s + bias[None].astype(jnp.float32)
        pr = jax.nn.softmax(s.reshape(B, N_HEADS_C, GRID_W, kr_n * NA_COLS), axis=-1).reshape(s.shape)
        return jnp.einsum('bhqrc,bhrqcd->bhqd', pr.astype(v.dtype), vr)
    o = lax.map(row_fn, jnp.arange(rows))
    return o.transpose(1, 0, 3, 2, 4).reshape(B, L, D_INNER)


def setup_inputs(seed: int = 0) -> dict:
    key = jax.random.key(seed)
    ks = jax.random.split(key, 22)
    f32 = jnp.float32
    def nrm(k, shape, s):
        return jax.random.normal(k, shape, f32) * s
    fo = HYENA_FILTER_ORDER
    return {
        'x': nrm(ks[0], (BATCH, SEQ, D_MODEL), 1.0),
        'p': nrm(ks[1], (DEPTH, BATCH, SEQ, PLE_DIM), 1.0),
        'w_in': nrm(ks[2], (DEPTH, D_MODEL, 4 * D_INNER), D_MODEL ** -0.5),
        'w_out': nrm(ks[3], (DEPTH, D_INNER, D_MODEL), D_INNER ** -0.5),
        'norm_pre': 1.0 + nrm(ks[4], (DEPTH, D_MODEL), 0.02),
        'norm_post': 1.0 + nrm(ks[5], (DEPTH, D_MODEL), 0.02),
        'hyena_conv_w': nrm(ks[6], (N_EVEN, HYENA_SHORT, 3 * HYENA_WIDTH), HYENA_SHORT ** -0.5),
        'hyena_conv_b': nrm(ks[7], (N_EVEN, 3 * HYENA_WIDTH), 0.02),
        'hyena_w1': nrm(ks[8], (N_EVEN, HYENA_EMB_DIM, fo), HYENA_EMB_DIM ** -0.5),
        'hyena_b1': nrm(ks[9], (N_EVEN, fo), 0.02),
        'hyena_w2': nrm(ks[10], (N_EVEN, fo, fo), fo ** -0.5),
        'hyena_b2': nrm(ks[11], (N_EVEN, fo), 0.02),
        'hyena_w3': nrm(ks[12], (N_EVEN, fo, fo), fo ** -0.5),
        'hyena_b3': nrm(ks[13], (N_EVEN, fo), 0.02),
        'hyena_w4': nrm(ks[14], (N_EVEN, fo, 4 * HYENA_WIDTH), fo ** -0.5),
        'hyena_freq': 1.0 + nrm(ks[15], (N_EVEN, fo), 0.02),
        'hyena_skip': nrm(ks[16], (N_EVEN, 2, HYENA_WIDTH), 0.5),
        'rel_bias': nrm(ks[17], (REL_BUCKETS, N_HEADS_B), 0.1),
        'na_rpb': nrm(ks[18], (N_ODD, N_HEADS_C, 2 * NA_ROWS_MAX - 1, 2 * NA_COLS - 1), 0.1),
        'ple_proj': nrm(ks[19], (DEPTH, PLE_DIM, D_MODEL), PLE_DIM ** -0.5),
        'ple_norm': 1.0 + nrm(ks[20], (DEPTH, D_MODEL), 0.02),
        'ple_gate': nrm(ks[21], (DEPTH, D_MODEL, D_MODEL), D_MODEL ** -0.5),
    }


def reference(x, p, w_in, w_out, norm_pre, norm_post, hyena_conv_w, hyena_conv_b,
              hyena_w1, hyena_b1, hyena_w2, hyena_b2, hyena_w3, hyena_b3, hyena_w4,
              hyena_freq, hyena_skip, rel_bias, na_rpb, ple_proj, ple_norm, ple_gate):
    h = x
    for i in range(DEPTH):
        j = i // 2
        u = _rmsnorm(h, norm_pre[i])
        proj = u @ w_in[i]
        if i % 2 == 0:
            a_in, a_gate, b_qkv, b_gate = jnp.split(
                proj, [3 * HYENA_WIDTH, 4 * HYENA_WIDTH, 4 * HYENA_WIDTH + 3 * ATTN_WIDTH], axis=-1)
            ya = _hyena(a_in, hyena_conv_w[j], hyena_conv_b[j], hyena_w1[j], hyena_b1[j],
                        hyena_w2[j], hyena_b2[j], hyena_w3[j], hyena_b3[j], hyena_w4[j],
                        hyena_freq[j], hyena_skip[j]) * jax.nn.silu(a_gate)
            yb = _dilated_attention(b_qkv, rel_bias) * jax.nn.silu(b_gate)
            y = jnp.concatenate([ya, yb], axis=-1)
        else:
            c_qkv, c_gate = jnp.split(proj, [3 * D_INNER], axis=-1)
            y = _neighbourhood_attention(c_qkv, na_rpb[j]) * jax.nn.silu(c_gate)
        h = h + _rmsnorm(y @ w_out[i], norm_post[i])
        e = _rmsnorm(p[i] @ ple_proj[i], ple_norm[i])
        h = h + jax.nn.sigmoid(h @ ple_gate[i]) * e
    return h
```

```python
import math
import contextlib
import numpy as np
import concourse.bass as bass
import concourse.mybir as mybir
from concourse.bass_utils import run_bass_kernel_spmd

F32 = mybir.dt.float32
BF16 = mybir.dt.bfloat16
ALU = mybir.AluOpType
AF = mybir.ActivationFunctionType
AX = mybir.AxisListType

L = 4096
D = 1024
DEPTH = 4
NT = 32
N_DMA_SEMS = 40


class Buf:
    __slots__ = ("name", "w", "r")

    def __init__(self, name):
        self.name = name
        self.w = None
        self.r = {}


class Sched:
    def __init__(self, nc):
        self.nc = nc
        self.eng = {"pe": nc.tensor, "dve": nc.vector, "act": nc.scalar, "pool": nc.gpsimd, "sp": nc.sync}
        self.sems = {}
        self.cnt = {}
        for k in ("pe", "dve", "act", "pool"):
            self.sems[k] = nc.alloc_semaphore(name=f"sem_{k}")
            self.cnt[k] = 0
        for i in range(N_DMA_SEMS):
            self.sems[("dma", i)] = nc.alloc_semaphore(name=f"sem_dma{i}")
        self.dma_i = 0
        self.seen = {e: {} for e in self.eng}
        self.bufs = {}
        self.n_inst = 0
        self.n_wait = 0

    def buf(self, name):
        b = self.bufs.get(name)
        if b is None:
            b = Buf(name)
            self.bufs[name] = b
        return b

    def _wait(self, e, ev):
        if ev is None:
            return
        k, v = ev
        if k == "pe" and e == "pe":
            return
        if self.seen[e].get(k, 0) >= v:
            return
        self.eng[e].wait_ge(self.sems[k], v)
        self.seen[e][k] = v
        self.n_wait += 1

    def _bl(self, names):
        return [self.buf(b) if isinstance(b, str) else b for b in names]

    def deps(self, e, reads, writes):
        for b in reads:
            self._wait(e, b.w)
        for b in writes:
            self._wait(e, b.w)
            for k, v in b.r.items():
                self._wait(e, (k, v))

    def mark(self, ev, reads, writes):
        k, v = ev
        for b in reads:
            if b.r.get(k, 0) < v:
                b.r[k] = v
        for b in writes:
            b.w = ev
            b.r = {}

    def op(self, e, fn, reads=(), writes=()):
        reads = self._bl(reads)
        writes = self._bl(writes)
        self.deps(e, reads, writes)
        ins = fn()
        self.cnt[e] += 1
        ins.then_inc(self.sems[e], 1)
        self.mark((e, self.cnt[e]), reads, writes)
        self.n_inst += 1
        return ins

    def dma(self, e, out, in_, reads=(), writes=(), **kw):
        reads = self._bl(reads)
        writes = self._bl(writes)
        i = self.dma_i
        self.dma_i += 1
        slot = i % N_DMA_SEMS
        rnd = i // N_DMA_SEMS
        key = ("dma", slot)
        if rnd > 0:
            self._wait(e, (key, 16 * rnd))
        self.deps(e, reads, writes)
        ins = self.eng[e].dma_start(out=out, in_=in_, **kw)
        ins.then_inc(self.sems[key], 16)
        ev = (key, 16 * (rnd + 1))
        self.mark(ev, reads, writes)
        self.n_inst += 1
        return ev

    def wait_all(self, e):
        for k in ("pe", "dve", "act", "pool"):
            if self.cnt[k] > 0:
                self._wait(e, (k, self.cnt[k]))
        for slot in range(min(self.dma_i, N_DMA_SEMS)):
            n = (self.dma_i - 1 - slot) // N_DMA_SEMS + 1
            self._wait(e, (("dma", slot), 16 * n))

    def barrier(self):
        for e in ("sp", "pool", "act", "dve", "pe"):
            self.wait_all(e)
        self.bufs = {}


def _consts():
    c = {}
    c["ident"] = np.eye(128, dtype=np.float32)
    return c


class Prog:
    def __init__(self, dbg=None):
        self.dbg = dbg or {}
        nc = bass.Bass("TRN2", target_bir_lowering=False)
        self.nc = nc
        self.S = Sched(nc)
        self.inputs = {}
        self.ps = [nc.alloc_psum_tensor(f"ps{i}", [128, 512], F32) for i in range(8)]
        self.psi = 0

    def din(self, name, shape, dt=F32):
        t = self.nc.dram_tensor(name, list(shape), dt, kind="ExternalInput")
        self.inputs[name] = t
        return t

    def dscratch(self, name, shape, dt):
        kind = "ExternalOutput" if name in self.dbg.get("outs", ()) else "Internal"
        if name in self.dbg.get("ins", ()):
            kind = "ExternalInput"
        return self.nc.dram_tensor(name, list(shape), dt, kind=kind)

    def next_ps(self):
        i = self.psi
        self.psi = (self.psi + 1) % 8
        return i

    def rot(self, key, banks):
        if not hasattr(self, "_rot"):
            self._rot = {}
        i = self._rot.get(key, 0)
        self._rot[key] = i + 1
        return banks[i % len(banks)]


def bcast_row(t, row_off, n, parts=128):
    return bass.AP(t, row_off, [[0, parts], [1, n]])


class Phase:
    _n = 0

    def __init__(self, P):
        self.P = P
        self.stack = contextlib.ExitStack()
        Phase._n += 1
        self.uid = Phase._n

    def __enter__(self):
        self.stack.__enter__()
        return self

    def __exit__(self, *a):
        self.P.S.barrier()
        return self.stack.__exit__(*a)

    def sb(self, name, shape, dt):
        return self.stack.enter_context(self.P.nc.sbuf_tensor(f"{name}_p{self.uid}", list(shape), dt))


def load_ident(P, ph):
    nc, S = P.nc, P.S
    idf = ph.sb("idf", [128, 128], F32)
    idb = ph.sb("idb", [128, 128], BF16)
    S.dma("sp", idf[:], P.c_ident.ap(), writes=["idf"])
    S.op("dve", lambda: nc.vector.tensor_copy(idb[:], idf[:]), reads=["idf"], writes=["idb"])
    return idf, idb


def rms_rstd(P, ms, tag):
    nc, S = P.nc, P.S
    S.op("dve", lambda: nc.vector.tensor_scalar(ms, ms, 1e-6, None, ALU.add), reads=[tag], writes=[tag])
    S.op("act", lambda: nc.scalar.sqrt(ms, ms), reads=[tag], writes=[tag])
    S.op("dve", lambda: nc.vector.reciprocal(ms, ms), reads=[tag], writes=[tag])


def phase_inproj(P, layer, hsrc):
    nc, S = P.nc, P.S
    even = layer % 2 == 0
    j = layer // 2
    with Phase(P) as ph:
        idf, idb = load_ident(P, ph)
        uT = ph.sb("uT", [128, 8, L], BF16)
        g = ph.sb("g_bc", [128, D], F32)
        S.dma("sp", g[:], bcast_row(P.t_norm_pre, layer * D, D), writes=["g"])
        xt = [ph.sb(f"xt{i}", [128, D], F32) for i in range(2)]
        sq = ph.sb("sq", [128, D], F32)
        ub = [ph.sb(f"ub{i}", [128, D], BF16) for i in range(2)]
        ms = [ph.sb(f"ms{i}", [128, 1], F32) for i in range(2)]
        for t in range(NT):
            b = t % 2
            S.dma("sp", xt[b][:], hsrc[t * 128:(t + 1) * 128, :], writes=[f"xt{b}"])
            S.op("act", lambda: nc.scalar.activation(sq[:], xt[b][:], AF.Square, scale=1.0 / 32, accum_out=ms[b][:]),
                 reads=[f"xt{b}"], writes=["sq", f"ms{b}"])
            rms_rstd(P, ms[b][:], f"ms{b}")
            S.op("dve", lambda: nc.vector.scalar_tensor_tensor(ub[b][:], xt[b][:], ms[b][:, 0:1], g[:], ALU.mult, ALU.mult),
                 reads=[f"xt{b}", f"ms{b}", "g"], writes=[f"ub{b}"])
            pi = P.next_ps()
            pst = P.ps[pi][:].bitcast(BF16)
            for k in range(8):
                S.op("pe", lambda: nc.tensor.transpose(pst[:, k * 128:(k + 1) * 128], ub[b][:, k * 128:(k + 1) * 128], idb[:]),
                     reads=[f"ub{b}", "idb"], writes=[f"ps{pi}"])
            S.op("pool" if False else "act", lambda: nc.scalar.copy(uT[:, :, t * 128:(t + 1) * 128], pst.rearrange("p (k n) -> p k n", k=8)),
                 reads=[f"ps{pi}"], writes=[f"uT{t // 4}"])

        wf = [ph.sb(f"wf{i}", [128, 8, 512], F32) for i in range(2)]
        wb = [ph.sb(f"wb{i}", [128, 8, 512], BF16) for i in range(2)]
        stg = [ph.sb(f"stg{i}", [128, L], F32) for i in range(2)]
        stb = [ph.sb(f"stb{i}", [128, L], BF16) for i in range(2)]
        vst = [ph.sb(f"vst{i}", [128, 512], BF16) for i in range(4)]
        if even:
            cw = ph.sb("cw", [128, 3, 24], F32)
            cb = ph.sb("cb", [128, 24], F32)
            for kk in range(3):
                src = bass.AP(P.t_conv_w, j * 3 * 3072 + kk * 3072, [[1, 128], [128, 24]])
                S.dma("sp", cw[:, kk, :], src, writes=["cw"], allow_slow_non_contiguous=True)
            src = bass.AP(P.t_conv_b, j * 3072, [[1, 128], [128, 24]])
            S.dma("sp", cb[:], src, writes=["cb"], allow_slow_non_contiguous=True)
        w_in = P.t_w_in.ap()[layer].rearrange("(k p) n -> p k n", p=128)
        n_st = 0
        n_vs = 0
        for blk in range(16):
            c0 = blk * 512
            wi = blk % 2
            S.dma("sp", wf[wi][:], w_in[:, :, c0:c0 + 512], writes=[f"wf{wi}"])
            S.op("pool", lambda: nc.gpsimd.tensor_copy(wb[wi][:], wf[wi][:]), reads=[f"wf{wi}"], writes=[f"wb{wi}"])
            if even:
                kind = "conv" if c0 < 3072 else "gate" if c0 < 4096 else "q" if c0 < 5120 else "k" if c0 < 6144 else "v" if c0 < 7168 else "gate"
            else:
                kind = "q" if c0 < 2048 else "k" if c0 < 4096 else "v" if c0 < 6144 else "gate"
            if kind == "v":
                vc0 = c0 - (6144 if even else 4096)
                for tt in range(NT):
                    pi = P.next_ps()
                    for k in range(8):
                        S.op("pe", lambda: nc.tensor.matmul(P.ps[pi][:], uT[:, k, tt * 128:(tt + 1) * 128], wb[wi][:, k, :], start=(k == 0), stop=(k == 7)),
                             reads=[f"uT{tt // 4}", f"wb{wi}"], writes=[f"ps{pi}"])
                    vb = n_vs % 4
                    n_vs += 1
                    if tt % 2 == 0:
                        S.op("act", lambda: nc.scalar.copy(vst[vb][:], P.ps[pi][:]), reads=[f"ps{pi}"], writes=[f"vst{vb}"])
                    else:
                        S.op("dve", lambda: nc.vector.tensor_copy(vst[vb][:], P.ps[pi][:]), reads=[f"ps{pi}"], writes=[f"vst{vb}"])
                    S.dma("pool", P.t_VTM.ap()[tt * 128:(tt + 1) * 128, vc0:vc0 + 512], vst[vb][:], reads=[f"vst{vb}"], writes=["VTM"])
                continue
            for ct in range(4):
                ch0 = c0 + ct * 128
                sb_i = n_st % 2
                n_st += 1
                for tg in range(8):
                    pi = P.next_ps()
                    for k in range(8):
                        S.op("pe", lambda: nc.tensor.matmul(P.ps[pi][:], wb[wi][:, k, ct * 128:(ct + 1) * 128], uT[:, k, tg * 512:(tg + 1) * 512], start=(k == 0), stop=(k == 7)),
                             reads=[f"uT{tg}", f"wb{wi}"], writes=[f"ps{pi}"])
                    sl = slice(tg * 512, (tg + 1) * 512)
                    if kind == "conv":
                        if tg % 2 == 0:
                            S.op("act", lambda: nc.scalar.copy(stg[sb_i][:, sl], P.ps[pi][:]), reads=[f"ps{pi}"], writes=[f"stg{sb_i}"])
                        else:
                            S.op("dve", lambda: nc.vector.tensor_copy(stg[sb_i][:, sl], P.ps[pi][:]), reads=[f"ps{pi}"], writes=[f"stg{sb_i}"])
                    elif kind == "gate":
                        S.op("act", lambda: nc.scalar.activation(stb[sb_i][:, sl], P.ps[pi][:], AF.Silu), reads=[f"ps{pi}"], writes=[f"stb{sb_i}"])
                    elif kind == "q":
                        S.op("act", lambda: nc.scalar.mul(stb[sb_i][:, sl], P.ps[pi][:], 0.125), reads=[f"ps{pi}"], writes=[f"stb{sb_i}"])
                    else:
                        if tg % 2 == 0:
                            S.op("act", lambda: nc.scalar.copy(stb[sb_i][:, sl], P.ps[pi][:]), reads=[f"ps{pi}"], writes=[f"stb{sb_i}"])
                        else:
                            S.op("dve", lambda: nc.vector.tensor_copy(stb[sb_i][:, sl], P.ps[pi][:]), reads=[f"ps{pi}"], writes=[f"stb{sb_i}"])
                if kind == "conv":
                    cti = ch0 // 128
                    s = stg[sb_i]
                    o = stb[sb_i]
                    S.op("act", lambda: nc.scalar.activation(o[:], s[:], AF.Identity, bias=cb[:, cti:cti + 1], scale=cw[:, 1, cti:cti + 1]),
                         reads=[f"stg{sb_i}", "cw", "cb"], writes=[f"stb{sb_i}"])
                    S.op("dve", lambda: nc.vector.scalar_tensor_tensor(o[:, 1:L], s[:, 0:L - 1], cw[:, 0, cti:cti + 1], o[:, 1:L], ALU.mult, ALU.add),
                         reads=[f"stg{sb_i}", "cw", f"stb{sb_i}"], writes=[f"stb{sb_i}"])
                    S.op("dve", lambda: nc.vector.scalar_tensor_tensor(o[:, 0:L - 1], s[:, 1:L], cw[:, 2, cti:cti + 1], o[:, 0:L - 1], ALU.mult, ALU.add),
                         reads=[f"stg{sb_i}", "cw", f"stb{sb_i}"], writes=[f"stb{sb_i}"])
                S.dma("pool", P.t_PT.ap()[ch0:ch0 + 128, :], stb[sb_i][:], reads=[f"stb{sb_i}"], writes=["PT"])


def phase_outproj(P, layer, hsrc, hdst):
    nc, S = P.nc, P.S
    with Phase(P) as ph:
        idf, idb = load_ident(P, ph)
        wo = ph.sb("wo", [128, 16, D], BF16)
        pg = ph.sb("pg", [128, 8, D], BF16)
        pp = ph.sb("pp", [128, 2, D], BF16)
        wtmp = [ph.sb(f"wtmp{i}", [128, 4, D], F32) for i in range(2)]
        gpost = ph.sb("gpost", [128, D], F32)
        gple = ph.sb("gple", [128, D], F32)
        S.dma("sp", gpost[:], bcast_row(P.t_norm_post, layer * D, D), writes=["gpost"])
        S.dma("sp", gple[:], bcast_row(P.t_ple_norm, layer * D, D), writes=["gple"])
        w_out = P.t_w_out.ap()[layer].rearrange("(k p) n -> p k n", p=128)
        w_pg = P.t_ple_gate.ap()[layer].rearrange("(k p) n -> p k n", p=128)
        w_pp = P.t_ple_proj.ap()[layer].rearrange("(k p) n -> p k n", p=128)
        nld = 0
        for c in range(4):
            b = nld % 2
            nld += 1
            S.dma("sp", wtmp[b][:], w_out[:, c * 4:(c + 1) * 4, :], writes=[f"wtmp{b}"])
            S.op("pool", lambda: nc.gpsimd.tensor_copy(wo[:, c * 4:(c + 1) * 4, :], wtmp[b][:]), reads=[f"wtmp{b}"], writes=["wo"])
        for c in range(2):
            b = nld % 2
            nld += 1
            S.dma("sp", wtmp[b][:], w_pg[:, c * 4:(c + 1) * 4, :], writes=[f"wtmp{b}"])
            S.op("pool", lambda: nc.gpsimd.tensor_copy(pg[:, c * 4:(c + 1) * 4, :], wtmp[b][:]), reads=[f"wtmp{b}"], writes=["pg"])
        b = nld % 2
        S.dma("sp", wtmp[b][:, 0:2, :], w_pp, writes=[f"wtmp{b}"])
        S.op("pool", lambda: nc.gpsimd.tensor_copy(pp[:], wtmp[b][:, 0:2, :]), reads=[f"wtmp{b}"], writes=["pp"])

        yt = [ph.sb(f"yt{i}", [128, 16, 128], BF16) for i in range(2)]
        ht = [ph.sb(f"ht{i}", [128, D], F32) for i in range(2)]
        ptl = [ph.sb(f"ptl{i}", [128, 256], F32) for i in range(2)]
        ptb_l = [ph.sb(f"ptb{i}", [128, 256], BF16) for i in range(2)]
        pT_l = [ph.sb(f"pT{i}", [128, 2, 128], BF16) for i in range(2)]
        sq_l = [ph.sb(f"sq{i}", [128, 512], F32) for i in range(2)]
        ms_l = [ph.sb(f"ms{i}", [128, 4], F32) for i in range(2)]
        rs_l = [ph.sb(f"rs{i}", [128, 2], F32) for i in range(2)]
        tmp_l = [ph.sb(f"tmp{i}", [128, D], F32) for i in range(2)]
        h1_l = [ph.sb(f"h1{i}", [128, D], F32) for i in range(2)]
        h1b_l = [ph.sb(f"h1b{i}", [128, D], BF16) for i in range(2)]
        h1T_l = [ph.sb(f"h1T{i}", [128, 8, 128], BF16) for i in range(2)]
        sg_l = [ph.sb(f"sg{i}", [128, D], F32) for i in range(2)]
        ee_l = [ph.sb(f"ee{i}", [128, D], F32) for i in range(2)]
        ho = [ph.sb(f"ho{i}", [128, D], F32) for i in range(2)]
        YT = P.t_YT.ap().rearrange("(k p) n -> p k n", p=128)
        pin = P.t_p.ap()[layer]
        for t in range(NT):
            b = t % 2
            tsl = slice(t * 128, (t + 1) * 128)
            ptb = ptb_l[b]
            pT = pT_l[b]
            sq = sq_l[b]
            ms = ms_l[b]
            rs = rs_l[b]
            tmp = tmp_l[b]
            h1 = h1_l[b]
            h1b = h1b_l[b]
            h1T = h1T_l[b]
            sg = sg_l[b]
            ee = ee_l[b]
            S.dma("sp", yt[b][:], YT[:, :, tsl], writes=[f"yt{b}"])
            S.dma("sp", ht[b][:], hsrc[tsl, :], writes=[f"ht{b}"])
            S.dma("sp", ptl[b][:], pin[tsl, :], writes=[f"ptl{b}"])
            po = [P.next_ps(), P.next_ps()]
            for half in range(2):
                pi = po[half]
                for k in range(16):
                    S.op("pe", lambda: nc.tensor.matmul(P.ps[pi][:], yt[b][:, k, :], wo[:, k, half * 512:(half + 1) * 512], start=(k == 0), stop=(k == 15)),
                         reads=[f"yt{b}", "wo"], writes=[f"ps{pi}"])
                S.op("act", lambda: nc.scalar.activation(sq[:], P.ps[pi][:], AF.Square, scale=1.0 / 32, accum_out=ms[:, half:half + 1]),
                     reads=[f"ps{pi}"], writes=[f"sq_{b}", f"ms_{b}"])
            S.op("dve", lambda: nc.vector.tensor_tensor(rs[:, 0:1], ms[:, 0:1], ms[:, 1:2], ALU.add), reads=[f"ms_{b}"], writes=[f"rs0_{b}"])
            rms_rstd(P, rs[:, 0:1], f"rs0_{b}")
            for half in range(2):
                pi = po[half]
                hs = slice(half * 512, (half + 1) * 512)
                S.op("dve", lambda: nc.vector.scalar_tensor_tensor(tmp[:, hs], P.ps[pi][:], rs[:, 0:1], gpost[:, hs], ALU.mult, ALU.mult),
                     reads=[f"ps{pi}", f"rs0_{b}", "gpost"], writes=[f"tmp_{b}"])
            S.op("pool", lambda: nc.gpsimd.tensor_tensor(h1[:], tmp[:], ht[b][:], ALU.add), reads=[f"tmp_{b}", f"ht{b}"], writes=[f"h1_{b}"])
            S.op("act", lambda: nc.scalar.copy(h1b[:], h1[:]), reads=[f"h1_{b}"], writes=[f"h1b_{b}"])
            pi = P.next_ps()
            pst = P.ps[pi][:].bitcast(BF16)
            for k in range(8):
                S.op("pe", lambda: nc.tensor.transpose(pst[:, k * 128:(k + 1) * 128], h1b[:, k * 128:(k + 1) * 128], idb[:]),
                     reads=[f"h1b_{b}", "idb"], writes=[f"ps{pi}"])
            S.op("dve", lambda: nc.vector.tensor_copy(h1T[:], pst.rearrange("p (k n) -> p k n", k=8)), reads=[f"ps{pi}"], writes=[f"h1T_{b}"])
            S.op("pool", lambda: nc.gpsimd.tensor_copy(ptb[:], ptl[b][:]), reads=[f"ptl{b}"], writes=[f"ptb_{b}"])
            pi = P.next_ps()
            pst2 = P.ps[pi][:].bitcast(BF16)
            for k in range(2):
                S.op("pe", lambda: nc.tensor.transpose(pst2[:, k * 128:(k + 1) * 128], ptb[:, k * 128:(k + 1) * 128], idb[:]),
                     reads=[f"ptb_{b}", "idb"], writes=[f"ps{pi}"])
            S.op("act", lambda: nc.scalar.copy(pT[:], pst2[:, 0:256].rearrange("p (k n) -> p k n", k=2)), reads=[f"ps{pi}"], writes=[f"pT_{b}"])
            for half in range(2):
                pi = P.next_ps()
                hs = slice(half * 512, (half + 1) * 512)
                for k in range(8):
                    S.op("pe", lambda: nc.tensor.matmul(P.ps[pi][:], h1T[:, k, :], pg[:, k, hs], start=(k == 0), stop=(k == 7)),
                         reads=[f"h1T_{b}", "pg"], writes=[f"ps{pi}"])
                S.op("act", lambda: nc.scalar.activation(sg[:, hs], P.ps[pi][:], AF.Sigmoid), reads=[f"ps{pi}"], writes=[f"sg_{b}"])
            pe_ = [P.next_ps(), P.next_ps()]
            for half in range(2):
                pi = pe_[half]
                hs = slice(half * 512, (half + 1) * 512)
                for k in range(2):
                    S.op("pe", lambda: nc.tensor.matmul(P.ps[pi][:], pT[:, k, :], pp[:, k, hs], start=(k == 0), stop=(k == 1)),
                         reads=[f"pT_{b}", "pp"], writes=[f"ps{pi}"])
                S.op("act", lambda: nc.scalar.activation(sq[:], P.ps[pi][:], AF.Square, scale=1.0 / 32, accum_out=ms[:, 2 + half:3 + half]),
                     reads=[f"ps{pi}"], writes=[f"sq_{b}", f"ms_{b}"])
            S.op("dve", lambda: nc.vector.tensor_tensor(rs[:, 1:2], ms[:, 2:3], ms[:, 3:4], ALU.add), reads=[f"ms_{b}"], writes=[f"rs1_{b}"])
            rms_rstd(P, rs[:, 1:2], f"rs1_{b}")
            for half in range(2):
                pi = pe_[half]
                hs = slice(half * 512, (half + 1) * 512)
                S.op("dve", lambda: nc.vector.scalar_tensor_tensor(ee[:, hs], P.ps[pi][:], rs[:, 1:2], gple[:, hs], ALU.mult, ALU.mult),
                     reads=[f"ps{pi}", f"rs1_{b}", "gple"], writes=[f"ee_{b}"])
            S.op("pool", lambda: nc.gpsimd.tensor_tensor(ee[:], ee[:], sg[:], ALU.mult), reads=[f"ee_{b}", f"sg_{b}"], writes=[f"ee_{b}"])
            S.op("pool", lambda: nc.gpsimd.tensor_tensor(ho[b][:], ee[:], h1[:], ALU.add), reads=[f"ee_{b}", f"h1_{b}"], writes=[f"ho{b}"])
            S.dma("pool", hdst[tsl, :], ho[b][:], reads=[f"ho{b}"], writes=["hdst"])


W_SPECS = [
    ("x", [L, D]), ("p", [DEPTH, L, 256]), ("w_in", [DEPTH, D, 8192]), ("w_out", [DEPTH, 2048, D]),
    ("norm_pre", [DEPTH, D]), ("norm_post", [DEPTH, D]),
    ("hyena_conv_w", [2, 3, 3072]), ("hyena_conv_b", [2, 3072]),
    ("hyena_w1", [2, 33, 64]), ("hyena_b1", [2, 64]), ("hyena_w2", [2, 64, 64]), ("hyena_b2", [2, 64]),
    ("hyena_w3", [2, 64, 64]), ("hyena_b3", [2, 64]), ("hyena_w4", [2, 64, 4096]), ("hyena_freq", [2, 64]),
    ("hyena_skip", [2, 2, 1024]), ("rel_bias", [32, 16]), ("na_rpb", [2, 32, 15, 31]),
    ("ple_proj", [DEPTH, 256, D]), ("ple_norm", [DEPTH, D]), ("ple_gate", [DEPTH, D, D]),
]


def declare_io(P):
    for name, shape in W_SPECS:
        setattr(P, "t_" + name.replace("hyena_", ""), P.din(name, shape))
    P.t_conv_w = P.t_conv_w
    P.c_ident = P.din("c_ident", [128, 128])
    P.t_PT = P.dscratch("PT", [8192, L], BF16)
    P.t_VTM = P.dscratch("VTM", [L, 2048], BF16)
    P.t_YT = P.dscratch("YT", [2048, L], BF16)
    P.t_HB = P.dscratch("HB", [L, D], F32)
    P.t_out = P.nc.dram_tensor("out", [L, D], F32, kind="ExternalOutput")
    P.t_na_rpb_f = P.din("na_rpb_f", [2, 32, 15, 31])
    P.t_GD = P.dscratch("GD", [16, 2304], BF16)
    P.t_GN = P.dscratch("GN", [32, 15 * 128], BF16)
    P.t_SKD = P.dscratch("SKD", [128, 16 * 2304], BF16)
    P.t_SKN = P.dscratch("SKN", [64, 32 * 15 * 128], BF16)
    P.na_tab = na_mask_table()
    P.t_KF = P.dscratch("KF", [2, 8, 128, 33 * 2 * 128], BF16)
    P.t_RN = P.dscratch("RN", [128, 16], F32)
    for name, shape in HY_CONST_SHAPES.items():
        setattr(P, name, P.din(name, shape))
    P.c_na_am = P.din("c_na_am", list(P.na_tab[0].shape))
    P.c_dil_oh = P.din("c_dil_oh", [32, 2303])
    P.c_dil_mult = P.din("c_dil_mult", [16, 2303])
    P.c_far_oh = P.din("c_far_oh", [32, 511])
    P.c_far_mult = P.din("c_far_mult", [16, 511])
    P.t_GF = P.dscratch("GF", [16, 512], BF16)
    P.t_SKF = P.dscratch("SKF", [128, 16 * 512], BF16)
    P.t_OF = P.dscratch("OF", [16, L, 65], F32)


def const_inputs():
    oh, mult = dil_tables()
    d = {"c_ident": np.eye(128, dtype=np.float32), "c_na_am": na_mask_table()[0], "c_dil_oh": oh, "c_dil_mult": mult}
    d["c_far_oh"], d["c_far_mult"] = far_tables()
    d.update(hyena_tables())
    return d


NA_CLASSES = [(0, [0, 1, 2, 3]), (1, [-1, 0, 1, 2]), (2, [-2, -1, 0, 1, 2]), (30, [-2, -1, 0, 1]), (31, [-3, -2, -1, 0])]


def na_class_of(qi):
    return 0 if qi == 0 else 1 if qi == 1 else 3 if qi == 30 else 4 if qi == 31 else 2


def na_mask_table():
    tiles, ds, starts = [], [], []
    kc = np.arange(64)
    cs = np.clip(kc - 8, 0, 48)
    colok = (kc[:, None] >= cs[None, :]) & (kc[:, None] < cs[None, :] + 16)
    for qi, dl in NA_CLASSES:
        starts.append(len(tiles))
        for d in dl:
            kt = qi + d
            m = np.zeros((128, 128), np.float32)
            for krl in range(2):
                for qrl in range(2):
                    kr, qr = 2 * kt + krl, 2 * qi + qrl
                    rs = min(max(qr - 4, 0), 56)
                    if rs <= kr < rs + 8:
                        m[krl * 64:(krl + 1) * 64, qrl * 64:(qrl + 1) * 64] = colok
            tiles.append(m)
            ds.append(d)
    AM = np.stack(tiles, axis=1)
    return np.ascontiguousarray(AM), ds, starts


def t5_bucket_np(rel):
    rel = np.asarray(rel, np.int64)
    ret = np.where(rel > 0, 16, 0)
    n = np.abs(rel)
    nf = np.maximum(n, 1).astype(np.float32)
    large = 8 + (np.log(nf / np.float32(8)) / np.float32(math.log(1024 / 8)) * np.float32(8)).astype(np.int32)
    large = np.minimum(large, 15)
    return ret + np.where(n < 8, n, large)


def dil_tables():
    delta = 1151 - np.arange(2303)
    b = t5_bucket_np(delta)
    OH = np.zeros((32, 2303), np.float32)
    OH[b, np.arange(2303)] = 1.0
    a = np.abs(delta)
    mult = (a <= 64).astype(np.float32) + ((delta % 4 == 0) & (a <= 256)) + ((delta % 16 == 0) & (a <= 256))
    MULT = np.tile(mult[None, :].astype(np.float32), (16, 1))
    return OH, MULT


def far_tables():
    dm = 255 - np.arange(511)
    b = t5_bucket_np(16 * dm)
    OH = np.zeros((32, 511), np.float32)
    OH[b, np.arange(511)] = 1.0
    a = np.abs(dm)
    mult = ((a >= 17) & (a <= 64)).astype(np.float32)
    return OH, np.tile(mult[None, :], (16, 1)).astype(np.float32)


def build_dil_table(P):
    nc, S = P.nc, P.S
    with Phase(P) as ph:
        rb = ph.sb("rb", [32, 16], F32)
        oh = ph.sb("oh", [32, 2304], F32)
        mu = ph.sb("mu", [16, 2304], F32)
        ge = ph.sb("ge", [16, 2304], F32)
        gb = ph.sb("gb", [16, 2304], BF16)
        S.dma("sp", rb[:], P.t_rel_bias.ap(), writes=["rb"])
        S.dma("sp", oh[:, 0:2303], P.c_dil_oh.ap(), writes=["oh"])
        S.dma("sp", mu[:, 0:2303], P.c_dil_mult.ap(), writes=["mu"])
        S.op("pool", lambda: nc.gpsimd.memset(gb[:], 0.0), writes=["gb"])
        for c in range(5):
            w = min(512, 2303 - c * 512)
            sl = slice(c * 512, c * 512 + w)
            pi = P.next_ps()
            S.op("pe", lambda: nc.tensor.matmul(P.ps[pi][0:16, 0:w], rb[:], oh[:, sl], start=True, stop=True), reads=["rb", "oh"], writes=[f"ps{pi}"])
            S.op("act", lambda: nc.scalar.activation(ge[:, sl], P.ps[pi][0:16, 0:w], AF.Exp), reads=[f"ps{pi}"], writes=["ge"])
            S.op("dve", lambda: nc.vector.tensor_tensor(gb[:, sl], ge[:, sl], mu[:, sl], ALU.mult), reads=["ge", "mu", "gb"], writes=["gb"])
        S.dma("sp", P.t_GD.ap(), gb[:], reads=["gb"], writes=["GD"])
        S.dma("sp", P.t_SKD.ap(), bass.AP(P.t_GD, 0, [[0, 128], [1, 16 * 2304]]), reads=["GD"], writes=["SKD"])
        S.dma("sp", oh[:, 0:511], P.c_far_oh.ap(), reads=["oh"], writes=["oh"])
        S.dma("sp", mu[:, 0:511], P.c_far_mult.ap(), reads=["mu"], writes=["mu"])
        gf = ph.sb("gf", [16, 512], BF16)
        S.op("pool", lambda: nc.gpsimd.memset(gf[:], 0.0), writes=["gf"])
        pi = P.next_ps()
        S.op("pe", lambda: nc.tensor.matmul(P.ps[pi][0:16, 0:511], rb[:], oh[:, 0:511], start=True, stop=True), reads=["rb", "oh"], writes=[f"ps{pi}"])
        S.op("act", lambda: nc.scalar.activation(ge[:, 0:511], P.ps[pi][0:16, 0:511], AF.Exp), reads=[f"ps{pi}", "gb"], writes=["ge"])
        S.op("dve", lambda: nc.vector.tensor_tensor(gf[:, 0:511], ge[:, 0:511], mu[:, 0:511], ALU.mult), reads=["ge", "mu", "gf"], writes=["gf"])
        S.dma("sp", P.t_GF.ap(), gf[:], reads=["gf"], writes=["GF"])
        S.dma("sp", P.t_SKF.ap(), bass.AP(P.t_GF, 0, [[0, 128], [1, 16 * 512]]), reads=["GF"], writes=["SKF"])


def build_na_table(P, j):
    nc, S = P.nc, P.S
    with Phase(P) as ph:
        rp = ph.sb("rp", [32, 15, 31], F32)
        gn = ph.sb("gn", [32, 15, 128], BF16)
        S.dma("sp", rp[:], P.t_na_rpb_f.ap()[j], writes=["rp"])
        S.op("pool", lambda: nc.gpsimd.memset(gn[:], 0.0), writes=["gn"])
        S.op("act", lambda: nc.scalar.activation(gn[:, :, 48:79], rp[:], AF.Exp), reads=["rp", "gn"], writes=["gn"])
        S.dma("sp", P.t_GN.ap().rearrange("h (r c) -> h r c", c=128), gn[:], reads=["gn"], writes=["GN"])
        S.dma("sp", P.t_SKN.ap(), bass.AP(P.t_GN, 0, [[0, 64], [1, 32 * 15 * 128]]), reads=["GN"], writes=["SKN"])


def phase_attention(P, layer):
    nc, S = P.nc, P.S
    even = layer % 2 == 0
    if even:
        npairs, q0, k0, g0, v0, y0 = 8, 4096, 5120, 7168, 0, 1024
        ND = 5
    else:
        npairs, q0, k0, g0, v0, y0 = 16, 0, 2048, 6144, 0, 0
        ND = 7
        AM_np, na_ds, na_starts = P.na_tab
        NM = AM_np.shape[1]
    PT = P.t_PT.ap()
    VTM = P.t_VTM.ap()
    with Phase(P) as ph:
        idf, idb = load_ident(P, ph)
        QT = [ph.sb(f"QT{i}", [128, L], BF16) for i in range(2)]
        KT = [ph.sb(f"KT{i}", [128, L], BF16) for i in range(2)]
        GT = [ph.sb(f"GT{i}", [128, L], BF16) for i in range(2)]
        V2 = [ph.sb(f"V2{i}", [128, NT, 130], BF16) for i in range(2)]
        YS = [ph.sb(f"YS{i}", [128, L], BF16) for i in range(2)]
        for i in range(2):
            S.op("pool", lambda: nc.gpsimd.memset(V2[i][:], 1.0), writes=[f"V2{i}"])
        EH = [ph.sb(f"EH{i}", [128, ND, 128], BF16) for i in range(4)]
        if even:
            V2R = [ph.sb(f"V2R{i}", [128, NT, 130], BF16) for i in range(2)]
            for i in range(2):
                S.op("pool", lambda: nc.gpsimd.memset(V2R[i][:], 1.0), writes=[f"V2R{i}"])
            EHF = [ph.sb(f"EHF{i}", [128, 3, 128], BF16) for i in range(4)]
            ofs = [ph.sb(f"ofs{i}", [128, 65], F32) for i in range(2)]
            ofl = [ph.sb(f"ofl{i}", [128, 2, 65], F32) for i in range(2)]
            otot = [ph.sb(f"otot{i}", [128, 130], F32) for i in range(2)]
        else:
            amf = ph.sb("amf", [128, NM, 128], F32)
            am = ph.sb("am", [128, NM, 128], BF16)
            S.dma("sp", amf[:], P.c_na_am.ap(), writes=["amf"])
            S.op("pool", lambda: nc.gpsimd.tensor_copy(am[:], amf[:]), reads=["amf"], writes=["am"])
            EF = [ph.sb(f"EF{i}", [128, NM, 128], BF16) for i in range(4)]
        PX = [ph.sb(f"PX{i}", [128, 5 * 128], BF16) for i in range(2)]
        PM = [ph.sb(f"PM{i}", [128, 5 * 128], BF16) for i in range(3)]
        OS = [ph.sb(f"OS{i}", [128, 128], BF16) for i in range(2)]
        rd = [ph.sb(f"rd{i}", [128, 2], F32) for i in range(2)]
        cnt = {"px": 0, "far": 0}
        for hp in range(npairs):
            pb = hp % 2
            ch = hp * 128
            S.dma("sp", QT[pb][:], PT[q0 + ch:q0 + ch + 128, :], writes=[f"QT{pb}"])
            S.dma("sp", KT[pb][:], PT[k0 + ch:k0 + ch + 128, :], writes=[f"KT{pb}"])
            S.dma("sp", GT[pb][:], PT[g0 + ch:g0 + ch + 128, :], writes=[f"GT{pb}"])
            for h in range(2):
                c0 = v0 + ch + h * 64
                src = VTM[:, c0:c0 + 64].rearrange("(t p) c -> p t c", p=128)
                S.dma("sp", V2[pb][:, :, h * 65:h * 65 + 64], src, writes=[f"V2{pb}"])
                if even:
                    for bb in range(2):
                        src = bass.AP(P.t_VTM, bb * 128 * 16 * 2048 + c0, [[16 * 2048, 128], [2048, 16], [1, 64]])
                        S.dma("sp", V2R[pb][:, bb * 16:(bb + 1) * 16, h * 65:h * 65 + 64], src, writes=[f"V2R{pb}"])
            for h in range(2):
                hd = hp * 2 + h
                ei = pb * 2 + h
                if even:
                    src = bass.AP(P.t_SKD, hd * 2304 + 127 + 128 * 6, [[16 * 2304 - 1, 128], [128, 5], [1, 128]])
                    S.dma("sp", EH[ei][:], src, writes=[f"EH{ei}"])
                    src = bass.AP(P.t_SKF, hd * 512 + 127, [[16 * 512 - 1, 128], [128, 3], [1, 128]])
                    S.dma("sp", EHF[ei][:], src, writes=[f"EHF{ei}"])
                else:
                    for krl in range(2):
                        for qrl in range(2):
                            off = (hd * 15 + (-6 + krl - qrl + 7)) * 128 + 63
                            src = bass.AP(P.t_SKN, off, [[32 * 15 * 128 - 1, 64], [2 * 128, 7], [1, 64]])
                            S.dma("sp", EH[ei][krl * 64:(krl + 1) * 64, :, qrl * 64:(qrl + 1) * 64], src, writes=[f"EH{ei}"])
                    for m in range(NM):
                        d = na_ds[m]
                        S.op("pool", lambda: nc.gpsimd.tensor_tensor(EF[ei][:, m, :], EH[ei][:, d + 3, :], am[:, m, :], ALU.mult),
                             reads=[f"EH{ei}", "am"], writes=[f"EF{ei}"])

            units = []
            if even:
                for h in range(2):
                    ei = pb * 2 + h
                    hs = slice(h * 64, (h + 1) * 64)
                    for r in range(16):
                        for bq in range(2):
                            kbs = [1, 0]
                            units.append(dict(kind="far", h=h, r=r, bq=bq,
                                              q=QT[pb][hs, r + 2048 * bq:2048 * (bq + 1):16],
                                              ks=[KT[pb][hs, r + 2048 * bk:2048 * (bk + 1):16] for bk in kbs],
                                              vs=[V2R[pb][:, bk * 16 + r, h * 65:(h + 1) * 65] for bk in kbs],
                                              mview=EHF[ei][:, 1 - (kbs[0] - bq):1 - (kbs[-1] - bq) + 1, :], mname=f"EHF{ei}", vname=f"V2R{pb}"))
            for qi in range(NT):
                qsl = slice(qi * 128, (qi + 1) * 128)
                for h in range(2):
                    ei = pb * 2 + h
                    hs = slice(h * 64, (h + 1) * 64)
                    if even:
                        kts = list(range(min(31, qi + 2), max(0, qi - 2) - 1, -1))
                        mview = EH[ei][:, 2 - (kts[0] - qi):2 - (kts[-1] - qi) + 1, :]
                        mname = f"EH{ei}"
                    else:
                        cl = na_class_of(qi)
                        dl = NA_CLASSES[cl][1]
                        kts = [qi + d for d in dl]
                        mview = EF[ei][:, na_starts[cl]:na_starts[cl] + len(dl), :]
                        mname = f"EF{ei}"
                    units.append(dict(kind="near", h=h, qi=qi, q=QT[pb][hs, qsl],
                                      ks=[KT[pb][hs, kt * 128:(kt + 1) * 128] for kt in kts],
                                      vs=[V2[pb][:, kt, h * 65:(h + 1) * 65] for kt in kts],
                                      mview=mview, mname=mname, vname=f"V2{pb}"))

            po_of = {}

            def stage_a(u):
                n = len(u["ks"])
                xb = cnt["px"] % 2
                mb = cnt["px"] % 3
                cnt["px"] += 1
                for g in range(0, n, 4):
                    gk = u["ks"][g:g + 4]
                    pi = P.rot("att_s", [0, 1, 2, 3, 4])
                    for i, kap in enumerate(gk):
                        S.op("pe", lambda: nc.tensor.matmul(P.ps[pi][:, i * 128:(i + 1) * 128], kap, u["q"], start=True, stop=True),
                             reads=[f"KT{pb}", f"QT{pb}"], writes=[f"ps{pi}"])
                    S.op("act", lambda: nc.scalar.activation(PX[xb][:, g * 128:(g + len(gk)) * 128], P.ps[pi][:, 0:len(gk) * 128], AF.Exp),
                         reads=[f"ps{pi}"], writes=[f"PX{xb}"])
                S.op("dve", lambda: nc.vector.tensor_tensor(PM[mb][:, 0:n * 128].rearrange("p (m q) -> p m q", q=128), PX[xb][:, 0:n * 128].rearrange("p (m q) -> p m q", q=128), u["mview"], ALU.mult),
                     reads=[f"PX{xb}", u["mname"]], writes=[f"PM{mb}"])
                return mb

            def stage_b(u, mb):
                n = len(u["ks"])
                h = u["h"]
                if u["kind"] == "far":
                    po = P.rot("att_o", [5, 6])
                    for i, vap in enumerate(u["vs"]):
                        S.op("pe", lambda: nc.tensor.matmul(P.ps[po][:, 0:65], PM[mb][:, i * 128:(i + 1) * 128], vap, start=(i == 0), stop=(i == n - 1)),
                             reads=[f"PM{mb}", u["vname"]], writes=[f"ps{po}"])
                    fb = cnt["far"] % 2
                    cnt["far"] += 1
                    S.op("act", lambda: nc.scalar.copy(ofs[fb][:], P.ps[po][:, 0:65]), reads=[f"ps{po}"], writes=[f"ofs{fb}"])
                    hg = hp * 2 + h
                    dst = bass.AP(P.t_OF, (hg * L + 16 * 128 * u["bq"] + u["r"]) * 65, [[16 * 65, 128], [1, 65]])
                    S.dma("pool", dst, ofs[fb][:], reads=[f"ofs{fb}"], writes=["OF"])
                    return
                qi = u["qi"]
                qsl = slice(qi * 128, (qi + 1) * 128)
                ob = qi % 2
                if h == 0:
                    po_of[qi] = P.rot("att_o", [5, 6])
                    if even:
                        src = P.t_OF.ap()[hp * 2:hp * 2 + 2, qsl, :].rearrange("h p d -> p h d")
                        S.dma("sp", ofl[ob][:], src, reads=["OF"], writes=[f"ofl{ob}"])
                po = po_of[qi]
                for i, vap in enumerate(u["vs"]):
                    S.op("pe", lambda: nc.tensor.matmul(P.ps[po][:, h * 65:(h + 1) * 65], PM[mb][:, i * 128:(i + 1) * 128], vap, start=(i == 0), stop=(i == n - 1)),
                         reads=[f"PM{mb}", u["vname"]], writes=[f"ps{po}"])
                if h == 0:
                    return
                if even:
                    S.op("dve", lambda: nc.vector.tensor_tensor(otot[ob][:], P.ps[po][:, 0:130], ofl[ob][:].rearrange("p h d -> p (h d)"), ALU.add),
                         reads=[f"ps{po}", f"ofl{ob}"], writes=[f"otot{ob}"])
                    osrc = otot[ob][:]
                    oname = f"otot{ob}"
                else:
                    osrc = P.ps[po][:, 0:130]
                    oname = f"ps{po}"
                S.op("dve", lambda: nc.vector.reciprocal(rd[ob][:], osrc[:, 64:130:65]), reads=[oname], writes=[f"rd{ob}"])
                rdb = rd[ob][:]
                rdb = bass.AP(rdb.tensor, rdb.offset, list(rdb.ap) + [[0, 64]])
                S.op("dve", lambda: nc.vector.tensor_tensor(OS[ob][:].rearrange("p (h d) -> p h d", d=64), osrc.rearrange("p (h d) -> p h d", d=65)[:, :, 0:64], rdb, ALU.mult),
                     reads=[oname, f"rd{ob}"], writes=[f"OS{ob}"])
                pt_ = 7
                ptv = P.ps[pt_][:].bitcast(BF16)[:, (qi % 4) * 128:(qi % 4 + 1) * 128]
                S.op("pe", lambda: nc.tensor.transpose(ptv[:, 0:128], OS[ob][:], idb[:]), reads=[f"OS{ob}", "idb"], writes=[f"ps{pt_}"])
                S.op("dve", lambda: nc.vector.tensor_tensor(YS[pb][:, qsl], ptv[:, 0:128], GT[pb][:, qsl], ALU.mult),
                     reads=[f"ps{pt_}", f"GT{pb}"], writes=[f"YS{pb}"])

            pending = None
            for u in units:
                mb = stage_a(u)
                if pending is not None:
                    stage_b(*pending)
                pending = (u, mb)
            stage_b(*pending)
            S.dma("pool", P.t_YT.ap()[y0 + ch:y0 + ch + 128, :], YS[pb][:], reads=[f"YS{pb}"], writes=["YT"])


I32 = mybir.dt.int32
NFFT = 8192


def hyena_tables():
    t = {}
    n2 = np.arange(64)[:, None].astype(np.float64)
    jj = np.arange(64)[None, :]
    t["c_f64"] = np.where(jj <= 32, np.cos(2 * np.pi * n2 * jj / 64), -np.sin(2 * np.pi * n2 * (jj - 32) / 64)).astype(np.float32)
    n1 = np.arange(128)[:, None, None]
    k2 = np.arange(33)[None, :, None]
    k1 = np.arange(128)[None, None, :]
    phi = 2 * np.pi * ((n1 * (64 * k1 + k2)) % NFFT) / NFFT
    t["c_mr"] = np.cos(phi).astype(np.float32).reshape(128, 33 * 128)
    t["c_mi"] = (-np.sin(phi)).astype(np.float32).reshape(128, 33 * 128)
    k1_ = np.arange(128)[:, None]
    n1_ = np.arange(128)[None, :]
    th = 2 * np.pi * ((n1_ * k1_) % 128) / 128
    t["c_cos"] = np.cos(th).astype(np.float32)
    t["c_sin"] = np.sin(th).astype(np.float32)
    k2 = np.arange(33)[:, None, None]
    n1 = np.arange(128)[None, :, None]
    n2 = np.arange(32)[None, None, :]
    w = np.where((k2 == 0) | (k2 == 32), 1.0, 2.0)
    psi = 2 * np.pi * (((n1 + 128 * n2) * k2) % NFFT) / NFFT
    nr = (w * np.cos(psi) / NFFT)
    ni = (-w * np.sin(psi) / NFFT)
    t["c_nri"] = np.stack([nr, ni], axis=1).astype(np.float32).reshape(66, 128 * 32)
    tt = np.linspace(0.0, 1.0, L)
    fr = np.linspace(1e-4, 15.0, 16)[:, None]
    wpos = (2.0 * np.pi / L) * np.arange(L)[None, :]
    z = np.concatenate([tt[None, :], np.cos(fr * wpos), -np.sin(fr * wpos)], axis=0)
    t["c_z"] = z.astype(np.float32)
    z2 = z.copy()
    z2[:, 1:] = z[:, :0:-1]
    t["c_z2"] = z2.astype(np.float32)
    mind = math.log(1e-2) / 1.5
    maxd = math.log(1e-2) / 0.3
    deltas = np.abs(np.linspace(mind, maxd, 1024))
    t["c_negdelta"] = np.ascontiguousarray((-deltas).reshape(8, 128).T).astype(np.float32)
    return t


HY_CONST_SHAPES = {"c_f64": [64, 64], "c_mr": [128, 33 * 128], "c_mi": [128, 33 * 128], "c_cos": [128, 128], "c_sin": [128, 128],
                   "c_nri": [66, 4096], "c_z": [33, L], "c_z2": [33, L], "c_negdelta": [128, 8]}


def hy_load_consts(P, ph, stage, stage_name, inverse):
    nc, S = P.nc, P.S
    C = {}

    def ld(name, parts, n, neg=False):
        t = ph.sb("k" + name, [parts, n], BF16)
        src = getattr(P, name).ap()
        for c0 in range(0, n, 2048):
            w = min(2048, n - c0)
            S.dma("sp", stage[0:parts, 0:w], src[:, c0:c0 + w], writes=[stage_name])
            S.op("pool", lambda: nc.gpsimd.tensor_copy(t[:, c0:c0 + w], stage[0:parts, 0:w]), reads=[stage_name], writes=["k" + name])
        return t

    C["f64"] = ld("c_f64", 64, 64)
    C["mr"] = ld("c_mr", 128, 33 * 128)
    C["mi"] = ld("c_mi", 128, 33 * 128)
    mn = ph.sb("kc_min", [128, 33 * 128], BF16)
    S.op("pool", lambda: nc.gpsimd.tensor_scalar(mn[:], C["mi"][:], -1.0, None, ALU.mult), reads=["kc_mi"], writes=["kc_min"])
    C["min"] = mn
    if inverse:
        C["cos"] = ld("c_cos", 128, 128)
        C["sin"] = ld("c_sin", 128, 128)
        ns = ph.sb("kc_nsin", [128, 128], BF16)
        S.op("pool", lambda: nc.gpsimd.tensor_scalar(ns[:], C["sin"][:], -1.0, None, ALU.mult), reads=["kc_sin"], writes=["kc_nsin"])
        C["nsin"] = ns
        C["nri"] = ld("c_nri", 66, 4096)
    return C


def hy_fwd(P, C, idb, sT, sT_name, XC, A_sb, A_name, Xout, Xout_name, nblk=32):
    nc, S = P.nc, P.S
    Xb = XC[0:nblk, :].rearrange("p (n c) -> p n c", c=128)
    nev = 0
    for g in range(16):
        pi = P.rot("hy_a", [0, 1, 2, 3])
        pv = P.ps[pi][:].bitcast(BF16)
        for i in range(8):
            n1 = g * 8 + i
            S.op("pe", lambda: nc.tensor.transpose(pv[0:nblk, i * 128:(i + 1) * 128], sT[:, n1:nblk * 128:128], idb[:]),
                 reads=[sT_name, "idb"], writes=[f"ps{pi}"])
        dst = XC[0:nblk, g * 1024:(g + 1) * 1024]
        if g % 2 == 0:
            S.op("act", lambda: nc.scalar.copy(dst, pv[0:nblk, :]), reads=[f"ps{pi}"], writes=["XC"])
        else:
            S.op("dve", lambda: nc.vector.tensor_copy(dst, pv[0:nblk, :]), reads=[f"ps{pi}"], writes=["XC"])
    for g in range(16):
        pi = P.rot("hy_a", [0, 1, 2, 3])
        for i in range(8):
            c = g * 8 + i
            S.op("pe", lambda: nc.tensor.matmul(P.ps[pi][:, i * 64:(i + 1) * 64], Xb[:, :, c], C["f64"][0:nblk, :], start=True, stop=True),
                 reads=["XC", "kc_f64"], writes=[f"ps{pi}"])
        dst = A_sb[:, :, g * 8:(g + 1) * 8]
        src = P.ps[pi][:].rearrange("p (c j) -> p j c", j=64)
        if g % 2 == 0:
            S.op("act", lambda: nc.scalar.copy(dst, src), reads=[f"ps{pi}"], writes=[A_name])
        else:
            S.op("dve", lambda: nc.vector.tensor_copy(dst, src), reads=[f"ps{pi}"], writes=[A_name])
    mr = C["mr"][:].rearrange("p (k m) -> p k m", m=128)
    mi = C["mi"][:].rearrange("p (k m) -> p k m", m=128)
    mn = C["min"][:].rearrange("p (k m) -> p k m", m=128)
    for kp in range(17):
        pi = P.rot("hy_b", [4, 5, 6, 7])
        ks = [k for k in (2 * kp, 2 * kp + 1) if k <= 32]
        for i, k2 in enumerate(ks):
            xr = P.ps[pi][:, (2 * i) * 128:(2 * i + 1) * 128]
            xi = P.ps[pi][:, (2 * i + 1) * 128:(2 * i + 2) * 128]
            ar = A_sb[:, k2, :]
            rd = [A_name, "kc_mr", "kc_mi", "kc_min"]
            if k2 in (0, 32):
                S.op("pe", lambda: nc.tensor.matmul(xr, mr[:, k2, :], ar, start=True, stop=True), reads=rd, writes=[f"ps{pi}"])
                S.op("pe", lambda: nc.tensor.matmul(xi, mi[:, k2, :], ar, start=True, stop=True), reads=rd, writes=[f"ps{pi}"])
            else:
                ai = A_sb[:, 32 + k2, :]
                S.op("pe", lambda: nc.tensor.matmul(xr, mr[:, k2, :], ar, start=True, stop=False), reads=rd, writes=[f"ps{pi}"])
                S.op("pe", lambda: nc.tensor.matmul(xr, mn[:, k2, :], ai, start=False, stop=True), reads=rd, writes=[f"ps{pi}"])
                S.op("pe", lambda: nc.tensor.matmul(xi, mi[:, k2, :], ar, start=True, stop=False), reads=rd, writes=[f"ps{pi}"])
                S.op("pe", lambda: nc.tensor.matmul(xi, mr[:, k2, :], ai, start=False, stop=True), reads=rd, writes=[f"ps{pi}"])
        nk = len(ks)
        dst = Xout[:, ks[0]:ks[0] + nk, :, :]
        src = P.ps[pi][:, 0:nk * 256].rearrange("p (k r c) -> p k r c", r=2, c=128)
        if kp % 2 == 0:
            S.op("act", lambda: nc.scalar.copy(dst, src), reads=[f"ps{pi}"], writes=[Xout_name])
        else:
            S.op("dve", lambda: nc.vector.tensor_copy(dst, src), reads=[f"ps{pi}"], writes=[Xout_name])


def hy_inv(P, C, idb, Yv, Y_name, Bs, B_name, XC, yT, yT_name):
    nc, S = P.nc, P.S
    Cv = XC[0:66, :].rearrange("p (c x) -> p c x", x=128)
    nri = C["nri"][:].rearrange("p (a b) -> p a b", b=32)
    yv = yT.rearrange("p (b a) -> p a b", a=128)
    Bf = Bs[:].rearrange("p k r c -> p (k r) c")
    for g in range(9):
        k0 = g * 4
        nk = min(4, 33 - k0)
        yr = Yv[:, k0:k0 + nk, 0, :]
        yi = Yv[:, k0:k0 + nk, 1, :]
        pr = P.rot("hy_b", [4, 5, 6, 7])
        orr = P.ps[pr][:, 0:nk * 128].rearrange("p (k c) -> p k c", c=128)
        S.op("pe", lambda: nc.tensor.matmul(orr, C["cos"][:], yr, start=True, stop=False), reads=[Y_name, "kc_cos"], writes=[f"ps{pr}"])
        S.op("pe", lambda: nc.tensor.matmul(orr, C["nsin"][:], yi, start=False, stop=True), reads=[Y_name, "kc_nsin"], writes=[f"ps{pr}"])
        pi_ = P.rot("hy_b", [4, 5, 6, 7])
        oi = P.ps[pi_][:, 0:nk * 128].rearrange("p (k c) -> p k c", c=128)
        S.op("pe", lambda: nc.tensor.matmul(oi, C["sin"][:], yr, start=True, stop=False), reads=[Y_name, "kc_sin"], writes=[f"ps{pi_}"])
        S.op("pe", lambda: nc.tensor.matmul(oi, C["cos"][:], yi, start=False, stop=True), reads=[Y_name, "kc_cos"], writes=[f"ps{pi_}"])
        S.op("act", lambda: nc.scalar.copy(Bs[:, k0:k0 + nk, 0, :], orr), reads=[f"ps{pr}"], writes=[B_name])
        S.op("dve", lambda: nc.vector.tensor_copy(Bs[:, k0:k0 + nk, 1, :], oi), reads=[f"ps{pi_}"], writes=[B_name])
    for g in range(16):
        pi = P.rot("hy_a", [0, 1, 2, 3])
        pv = P.ps[pi][:].bitcast(BF16)
        for i in range(8):
            c = g * 8 + i
            S.op("pe", lambda: nc.tensor.transpose(pv[0:66, i * 128:(i + 1) * 128], Bf[:, :, c], idb[:]), reads=[B_name, "idb"], writes=[f"ps{pi}"])
        dst = XC[0:66, g * 1024:(g + 1) * 1024]
        if g % 2 == 0:
            S.op("act", lambda: nc.scalar.copy(dst, pv[0:66, :]), reads=[f"ps{pi}"], writes=["XC"])
        else:
            S.op("dve", lambda: nc.vector.tensor_copy(dst, pv[0:66, :]), reads=[f"ps{pi}"], writes=["XC"])
    for g in range(8):
        pi = P.rot("hy_b", [4, 5, 6, 7])
        for i in range(16):
            n1 = g * 16 + i
            o = P.ps[pi][:, i * 32:(i + 1) * 32]
            S.op("pe", lambda: nc.tensor.matmul(o, Cv[:, :, n1], nri[:, n1, :], start=True, stop=True), reads=["XC", "kc_nri"], writes=[f"ps{pi}"])
        dst = yv[:, g * 16:(g + 1) * 16, :]
        src = P.ps[pi][:].rearrange("p (a b) -> p a b", b=32)
        if g % 2 == 0:
            S.op("act", lambda: nc.scalar.copy(dst, src), reads=[f"ps{pi}"], writes=[yT_name])
        else:
            S.op("dve", lambda: nc.vector.tensor_copy(dst, src), reads=[f"ps{pi}"], writes=[yT_name])


def phase_hyena_filters(P, layer):
    nc, S = P.nc, P.S
    j = layer // 2
    TWO_PI = 2.0 * math.pi
    with Phase(P) as ph:
        idf, idb = load_ident(P, ph)
        B = [ph.sb(f"B{i}", [128, L], F32) for i in range(6)]
        C = hy_load_consts(P, ph, B[0], "B0", inverse=False)
        XC = ph.sb("XC", [128, 16384], BF16)
        A_sb = ph.sb("A_sb", [128, 64, 128], BF16)
        X0 = ph.sb("X0", [128, 33, 2, 128], BF16)
        w1 = ph.sb("w1", [33, 64], F32)
        w2 = ph.sb("w2", [64, 64], F32)
        w3 = ph.sb("w3", [64, 64], F32)
        bq = ph.sb("bq", [64, 8], F32)
        w4s = [ph.sb(f"w4s{i}", [64, 128], F32) for i in range(2)]
        nd = ph.sb("nd", [128, 8], F32)
        sm = ph.sb("sm", [128, 2, 16], F32)
        S.dma("sp", w1[:], P.t_w1.ap()[j], writes=["w1"])
        S.dma("sp", w2[:], P.t_w2.ap()[j], writes=["w2"])
        S.dma("sp", w3[:], P.t_w3.ap()[j], writes=["w3"])
        for i, t in enumerate([P.t_b1, P.t_b2, P.t_b3, P.t_freq]):
            S.dma("sp", bq[:, i:i + 1], bass.AP(t, j * 64, [[1, 64], [1, 1]]), writes=["bq"])
        S.dma("sp", nd[:], P.c_negdelta.ap(), writes=["nd"])
        S.op("dve", lambda: nc.vector.tensor_scalar(bq[:, 4:5], bq[:, 3:4], 1.0 / TWO_PI, None, ALU.mult), reads=["bq"], writes=["bq"])
        for i in range(3):
            S.op("dve", lambda: nc.vector.tensor_scalar(bq[:, 5 + i:6 + i], bq[:, i:i + 1], bq[:, 4:5], 8.5, ALU.mult, ALU.add), reads=["bq"], writes=["bq"])

        def mlp(inp, iname, K, w, wname, bi, out, oname):
            it = B[2][0:64, :].bitcast(I32)
            mt = B[3][0:64, :]
            for c in range(8):
                sl = slice(c * 512, (c + 1) * 512)
                pi = P.rot("hy_a", [0, 1, 2, 3])
                S.op("pe", lambda: nc.tensor.matmul(P.ps[pi][0:64, :], w[0:K, :], inp[0:K, sl], start=True, stop=True), reads=[iname, wname], writes=[f"ps{pi}"])
                S.op("dve", lambda: nc.vector.tensor_scalar(out[0:64, sl], P.ps[pi][0:64, :], bq[:, 4:5], bq[:, 5 + bi:6 + bi], ALU.mult, ALU.add),
                     reads=[f"ps{pi}", "bq"], writes=[oname])
            o = out[0:64, :]
            S.op("dve", lambda: nc.vector.tensor_copy(it, o), reads=[oname], writes=["B2"])
            S.op("dve", lambda: nc.vector.tensor_tensor(o, o, it, ALU.subtract), reads=[oname, "B2"], writes=[oname])
            S.op("dve", lambda: nc.vector.tensor_scalar(mt, o, 0.0, None, ALU.is_lt), reads=[oname], writes=["B3"])
            S.op("pool", lambda: nc.gpsimd.tensor_tensor(o, o, mt, ALU.add), reads=[oname, "B3"], writes=[oname])
            S.op("act", lambda: nc.scalar.activation(o, o, AF.Sin, bias=-math.pi, scale=TWO_PI), reads=[oname], writes=[oname])

        for zsrc, a3i in ((P.c_z, 4), (P.c_z2, 5)):
            S.dma("sp", B[0][0:33, :], zsrc.ap(), writes=["B0"])
            mlp(B[0], "B0", 33, w1, "w1", 0, B[1], "B1")
            mlp(B[1], "B1", 64, w2, "w2", 1, B[0], "B0")
            mlp(B[0], "B0", 64, w3, "w3", 2, B[a3i], f"B{a3i}")
        S.dma("sp", B[0][:], bcast_row(P.c_z, 0, L), writes=["B0"])
        S.dma("sp", B[2][:], bcast_row(P.c_z2, 0, L), writes=["B2"])
        kc = B[3][:].bitcast(BF16)
        junk = A_sb[:].rearrange("p j c -> p (j c)")
        w4 = P.t_w4.ap()[j]
        for ct in range(8):
            for f in range(2):
                for dr in range(2):
                    tl, tln = (B[0], "B0") if dr == 0 else (B[2], "B2")
                    a3x, a3n = (B[4], "B4") if dr == 0 else (B[5], "B5")
                    S.op("act", lambda: nc.scalar.activation(B[1][:], tl[:], AF.Exp, scale=nd[:, ct:ct + 1]), reads=[tln, "nd"], writes=["B1"])
                    ctp = f * 16 + dr * 8 + ct
                    S.dma("sp", w4s[dr][:], w4[:, ctp * 128:(ctp + 1) * 128], writes=[f"w4s{dr}"])
                    for c in range(8):
                        sl = slice(c * 512, (c + 1) * 512)
                        osl = slice(dr * L + c * 512, dr * L + (c + 1) * 512)
                        pi = P.rot("hy_a", [0, 1, 2, 3])
                        S.op("pe", lambda: nc.tensor.matmul(P.ps[pi][:], w4s[dr][:], a3x[0:64, sl], start=True, stop=True), reads=[a3n, f"w4s{dr}"], writes=[f"ps{pi}"])
                        S.op("dve", lambda: nc.vector.tensor_tensor(kc[:, osl], P.ps[pi][:], B[1][:, sl], ALU.mult), reads=[f"ps{pi}", "B1"], writes=["B3"])
                S.op("pool", lambda: nc.gpsimd.memset(kc[:, L:L + 1], 0.0), writes=["B3"])
                col = f * 8 + ct
                S.op("act", lambda: nc.scalar.activation(junk, kc, AF.Abs, accum_out=sm[:, 0, col:col + 1]), reads=["B3"], writes=["A_sb", "sm"])
                S.op("dve", lambda: nc.vector.reciprocal(sm[:, 1, col:col + 1], sm[:, 0, col:col + 1]), reads=["sm"], writes=["sm"])
                hy_fwd(P, C, idb, kc, "B3", XC, A_sb, "A_sb", X0, "X0", nblk=64)
                S.dma("sp", P.t_KF.ap()[f, ct], X0[:].rearrange("p k r c -> p (k r c)"), reads=["X0"], writes=["KF"])
        S.dma("sp", P.t_RN.ap(), sm[:, 1, :], reads=["sm"], writes=["RN"])


def phase_hyena_conv(P, layer):
    nc, S = P.nc, P.S
    j = layer // 2
    PT = P.t_PT.ap()
    with Phase(P) as ph:
        idf, idb = load_ident(P, ph)
        yT = ph.sb("yT", [128, L], F32)
        C = hy_load_consts(P, ph, yT, "yT", inverse=True)
        XC = ph.sb("XC", [128, 16384], BF16)
        AY = ph.sb("AY", [128, 33 * 2 * 128], BF16)
        A_sb = AY[:, 0:8192].rearrange("p (j c) -> p j c", c=128)
        Yv = AY[:].rearrange("p (k r c) -> p k r c", r=2, c=128)
        Xs = ph.sb("Xs", [128, 33, 2, 128], BF16)
        Kf = ph.sb("Kf", [128, 33, 2, 128], BF16)
        T1 = ph.sb("T1", [128, 33, 128], BF16)
        T2 = ph.sb("T2", [128, 33, 128], BF16)
        vT = ph.sb("vT", [128, L], BF16)
        x1T = ph.sb("x1T", [128, L], BF16)
        zT = ph.sb("zT", [128, L], BF16)
        ost = ph.sb("ost", [128, L], BF16)
        sk = ph.sb("sk", [128, 2, 8], F32)
        rn = ph.sb("rn", [128, 16], F32)
        S.dma("sp", rn[:], P.t_RN.ap(), writes=["rn"])
        for f in range(2):
            S.dma("sp", sk[:, f, :], bass.AP(P.t_skip, (j * 2 + f) * 1024, [[1, 128], [128, 8]]), writes=["sk"], allow_slow_non_contiguous=True)

        def conv(f, ct, sT, sname):
            S.dma("sp", Kf[:].rearrange("p k r c -> p (k r c)"), P.t_KF.ap()[f, ct], writes=["Kf"])
            hy_fwd(P, C, idb, sT, sname, XC, A_sb, "AY", Xs, "Xs")
            xr, xi = Xs[:, :, 0, :], Xs[:, :, 1, :]
            kr, ki = Kf[:, :, 0, :], Kf[:, :, 1, :]
            S.op("dve", lambda: nc.vector.tensor_tensor(T1[:], xr, kr, ALU.mult), reads=["Xs", "Kf"], writes=["T1"])
            S.op("pool", lambda: nc.gpsimd.tensor_tensor(T2[:], xi, ki, ALU.mult), reads=["Xs", "Kf"], writes=["T2"])
            S.op("dve", lambda: nc.vector.tensor_tensor(Yv[:, :, 0, :], T1[:], T2[:], ALU.subtract), reads=["T1", "T2"], writes=["AY"])
            S.op("pool", lambda: nc.gpsimd.tensor_tensor(T1[:], xr, ki, ALU.mult), reads=["Xs", "Kf"], writes=["T1"])
            S.op("dve", lambda: nc.vector.tensor_tensor(T2[:], xi, kr, ALU.mult), reads=["Xs", "Kf"], writes=["T2"])
            S.op("pool", lambda: nc.gpsimd.tensor_tensor(Yv[:, :, 1, :], T1[:], T2[:], ALU.add), reads=["T1", "T2"], writes=["AY"])
            hy_inv(P, C, idb, Yv, "AY", Xs, "Xs", XC, yT[:], "yT")
            S.op("act", lambda: nc.scalar.activation(yT[:], yT[:], AF.Copy, scale=rn[:, f * 8 + ct:f * 8 + ct + 1]), reads=["yT", "rn"], writes=["yT"])

        for ct in range(8):
            r0 = ct * 128
            S.dma("sp", vT[:], PT[r0:r0 + 128, :], writes=["vT"])
            S.dma("sp", x1T[:], PT[1024 + r0:1024 + r0 + 128, :], writes=["x1T"])
            conv(0, ct, vT[:], "vT")
            S.op("dve", lambda: nc.vector.scalar_tensor_tensor(yT[:], vT[:], sk[:, 0, ct:ct + 1], yT[:], ALU.mult, ALU.add), reads=["vT", "sk", "yT"], writes=["yT"])
            S.op("pool", lambda: nc.gpsimd.tensor_tensor(zT[:], yT[:], x1T[:], ALU.mult), reads=["yT", "x1T"], writes=["zT"])
            S.dma("sp", vT[:], PT[2048 + r0:2048 + r0 + 128, :], writes=["vT"])
            S.dma("sp", x1T[:], PT[3072 + r0:3072 + r0 + 128, :], writes=["x1T"])
            conv(1, ct, zT[:], "zT")
            S.op("dve", lambda: nc.vector.scalar_tensor_tensor(yT[:], zT[:], sk[:, 1, ct:ct + 1], yT[:], ALU.mult, ALU.add), reads=["zT", "sk", "yT"], writes=["yT"])
            S.op("pool", lambda: nc.gpsimd.tensor_tensor(yT[:], yT[:], vT[:], ALU.mult), reads=["yT", "vT"], writes=["yT"])
            S.op("dve", lambda: nc.vector.tensor_tensor(ost[:], yT[:], x1T[:], ALU.mult), reads=["yT", "x1T"], writes=["ost"])
            S.dma("pool", P.t_YT.ap()[r0:r0 + 128, :], ost[:], reads=["ost"], writes=["YT"])


def phase_hyena(P, layer):
    phase_hyena_filters(P, layer)
    phase_hyena_conv(P, layer)


def build_program(dbg=None, layers=range(DEPTH)):
    P = Prog(dbg=dbg)
    declare_io(P)
    P.t_HB2 = P.dscratch("HB2", [L, D], F32)
    hb = [P.t_HB.ap(), P.t_HB2.ap()]
    layers = list(layers)
    built_dil = False
    for li, layer in enumerate(layers):
        hsrc = P.t_x.ap() if li == 0 else hb[(li - 1) % 2]
        hdst = P.t_out.ap() if li == len(layers) - 1 else hb[li % 2]
        phase_inproj(P, layer, hsrc)
        if layer % 2 == 0:
            if not built_dil:
                build_dil_table(P)
                built_dil = True
            phase_attention(P, layer)
            phase_hyena(P, layer)
        else:
            build_na_table(P, layer // 2)
            phase_attention(P, layer)
        phase_outproj(P, layer, hsrc, hdst)
    P.S.wait_all("sp")
    return P


_CACHE = {}


def kernel(**inputs):
    if "P" not in _CACHE:
        _CACHE["P"] = build_program()
        _CACHE["consts"] = const_inputs()
    P = _CACHE["P"]
    consts = _CACHE["consts"]
    B = inputs["x"].shape[0]
    shared = {}
    for name, shape in W_SPECS:
        if name in ("x", "p"):
            continue
        shared[name] = np.ascontiguousarray(inputs[name], dtype=np.float32)
    shared["na_rpb_f"] = np.ascontiguousarray(np.asarray(inputs["na_rpb"], dtype=np.float32)[..., ::-1])
    shared.update(consts)
    in_maps = []
    for b in range(B):
        m = dict(shared)
        m["x"] = np.ascontiguousarray(inputs["x"][b], dtype=np.float32)
        m["p"] = np.ascontiguousarray(np.asarray(inputs["p"])[:, b], dtype=np.float32)
        in_maps.append({k: v for k, v in m.items() if k in P.inputs})
    res = run_bass_kernel_spmd(P.nc, in_maps, core_ids=list(range(B)))
    out = np.stack([np.asarray(r["out"], dtype=np.float32) for r in res.results], axis=0)
    return out
```

```python
import math
import contextlib
import numpy as np
import concourse.bass as bass
import concourse.mybir as mybir
from concourse.bass_utils import run_bass_kernel_spmd

F32 = mybir.dt.float32
BF16 = mybir.dt.bfloat16
ALU = mybir.AluOpType
AF = mybir.ActivationFunctionType
AX = mybir.AxisListType

L = 4096
D = 1024
DEPTH = 4
NT = 32
N_DMA_SEMS = 40


class Buf:
    __slots__ = ("name", "w", "r")

    def __init__(self, name):
        self.name = name
        self.w = None
        self.r = {}


class Sched:
    def __init__(self, nc):
        self.nc = nc
        self.eng = {"pe": nc.tensor, "dve": nc.vector, "act": nc.scalar, "pool": nc.gpsimd, "sp": nc.sync}
        self.sems = {}
        self.cnt = {}
        for k in ("pe", "dve", "act", "pool"):
            self.sems[k] = nc.alloc_semaphore(name=f"sem_{k}")
            self.cnt[k] = 0
        for i in range(N_DMA_SEMS):
            self.sems[("dma", i)] = nc.alloc_semaphore(name=f"sem_dma{i}")
        self.dma_i = 0
        self.seen = {e: {} for e in self.eng}
        self.bufs = {}
        self.n_inst = 0
        self.n_wait = 0

    def buf(self, name):
        b = self.bufs.get(name)
        if b is None:
            b = Buf(name)
            self.bufs[name] = b
        return b

    def _wait(self, e, ev):
        if ev is None:
            return
        k, v = ev
        if k == "pe" and e == "pe":
            return
        if self.seen[e].get(k, 0) >= v:
            return
        self.eng[e].wait_ge(self.sems[k], v)
        self.seen[e][k] = v
        self.n_wait += 1

    def _bl(self, names):
        return [self.buf(b) if isinstance(b, str) else b for b in names]

    def deps(self, e, reads, writes):
        for b in reads:
            self._wait(e, b.w)
        for b in writes:
            self._wait(e, b.w)
            for k, v in b.r.items():
                self._wait(e, (k, v))

    def mark(self, ev, reads, writes):
        k, v = ev
        for b in reads:
            if b.r.get(k, 0) < v:
                b.r[k] = v
        for b in writes:
            b.w = ev
            b.r = {}

    def op(self, e, fn, reads=(), writes=()):
        reads = self._bl(reads)
        writes = self._bl(writes)
        self.deps(e, reads, writes)
        ins = fn()
        self.cnt[e] += 1
        ins.then_inc(self.sems[e], 1)
        self.mark((e, self.cnt[e]), reads, writes)
        self.n_inst += 1
        return ins

    def dma(self, e, out, in_, reads=(), writes=(), **kw):
        reads = self._bl(reads)
        writes = self._bl(writes)
        i = self.dma_i
        self.dma_i += 1
        slot = i % N_DMA_SEMS
        rnd = i // N_DMA_SEMS
        key = ("dma", slot)
        if rnd > 0:
            self._wait(e, (key, 16 * rnd))
        self.deps(e, reads, writes)
        ins = self.eng[e].dma_start(out=out, in_=in_, **kw)
        ins.then_inc(self.sems[key], 16)
        ev = (key, 16 * (rnd + 1))
        self.mark(ev, reads, writes)
        self.n_inst += 1
        return ev

    def wait_all(self, e):
        for k in ("pe", "dve", "act", "pool"):
            if self.cnt[k] > 0:
                self._wait(e, (k, self.cnt[k]))
        for slot in range(min(self.dma_i, N_DMA_SEMS)):
            n = (self.dma_i - 1 - slot) // N_DMA_SEMS + 1
            self._wait(e, (("dma", slot), 16 * n))

    def barrier(self):
        for e in ("sp", "pool", "act", "dve", "pe"):
            self.wait_all(e)
        self.bufs = {}


def _consts():
    c = {}
    c["ident"] = np.eye(128, dtype=np.float32)
    return c


class Prog:
    def __init__(self, dbg=None):
        self.dbg = dbg or {}
        nc = bass.Bass("TRN2", target_bir_lowering=False)
        self.nc = nc
        self.S = Sched(nc)
        self.inputs = {}
        self.ps = [nc.alloc_psum_tensor(f"ps{i}", [128, 512], F32) for i in range(8)]
        self.psi = 0

    def din(self, name, shape, dt=F32):
        t = self.nc.dram_tensor(name, list(shape), dt, kind="ExternalInput")
        self.inputs[name] = t
        return t

    def dscratch(self, name, shape, dt):
        kind = "ExternalOutput" if name in self.dbg.get("outs", ()) else "Internal"
        if name in self.dbg.get("ins", ()):
            kind = "ExternalInput"
        return self.nc.dram_tensor(name, list(shape), dt, kind=kind)

    def next_ps(self):
        i = self.psi
        self.psi = (self.psi + 1) % 8
        return i

    def rot(self, key, banks):
        if not hasattr(self, "_rot"):
            self._rot = {}
        i = self._rot.get(key, 0)
        self._rot[key] = i + 1
        return banks[i % len(banks)]


def bcast_row(t, row_off, n, parts=128):
    return bass.AP(t, row_off, [[0, parts], [1, n]])


class Phase:
    _n = 0

    def __init__(self, P):
        self.P = P
        self.stack = contextlib.ExitStack()
        Phase._n += 1
        self.uid = Phase._n

    def __enter__(self):
        self.stack.__enter__()
        return self

    def __exit__(self, *a):
        self.P.S.barrier()
        return self.stack.__exit__(*a)

    def sb(self, name, shape, dt):
        return self.stack.enter_context(self.P.nc.sbuf_tensor(f"{name}_p{self.uid}", list(shape), dt))


def load_ident(P, ph):
    nc, S = P.nc, P.S
    idf = ph.sb("idf", [128, 128], F32)
    idb = ph.sb("idb", [128, 128], BF16)
    S.dma("sp", idf[:], P.c_ident.ap(), writes=["idf"])
    S.op("dve", lambda: nc.vector.tensor_copy(idb[:], idf[:]), reads=["idf"], writes=["idb"])
    return idf, idb


def rms_rstd(P, ms, tag):
    nc, S = P.nc, P.S
    S.op("dve", lambda: nc.vector.tensor_scalar(ms, ms, 1e-6, None, ALU.add), reads=[tag], writes=[tag])
    S.op("act", lambda: nc.scalar.sqrt(ms, ms), reads=[tag], writes=[tag])
    S.op("dve", lambda: nc.vector.reciprocal(ms, ms), reads=[tag], writes=[tag])


def phase_inproj(P, layer, hsrc):
    nc, S = P.nc, P.S
    even = layer % 2 == 0
    j = layer // 2
    with Phase(P) as ph:
        idf, idb = load_ident(P, ph)
        uT = ph.sb("uT", [128, 8, L], BF16)
        g = ph.sb("g_bc", [128, D], F32)
        S.dma("sp", g[:], bcast_row(P.t_norm_pre, layer * D, D), writes=["g"])
        xt = [ph.sb(f"xt{i}", [128, D], F32) for i in range(2)]
        sq = ph.sb("sq", [128, D], F32)
        ub = [ph.sb(f"ub{i}", [128, D], BF16) for i in range(2)]
        ms = [ph.sb(f"ms{i}", [128, 1], F32) for i in range(2)]
        for t in range(NT):
            b = t % 2
            S.dma("sp", xt[b][:], hsrc[t * 128:(t + 1) * 128, :], writes=[f"xt{b}"])
            S.op("act", lambda: nc.scalar.activation(sq[:], xt[b][:], AF.Square, scale=1.0 / 32, accum_out=ms[b][:]),
                 reads=[f"xt{b}"], writes=["sq", f"ms{b}"])
            rms_rstd(P, ms[b][:], f"ms{b}")
            S.op("dve", lambda: nc.vector.scalar_tensor_tensor(ub[b][:], xt[b][:], ms[b][:, 0:1], g[:], ALU.mult, ALU.mult),
                 reads=[f"xt{b}", f"ms{b}", "g"], writes=[f"ub{b}"])
            pi = P.next_ps()
            pst = P.ps[pi][:].bitcast(BF16)
            for k in range(8):
                S.op("pe", lambda: nc.tensor.transpose(pst[:, k * 128:(k + 1) * 128], ub[b][:, k * 128:(k + 1) * 128], idb[:]),
                     reads=[f"ub{b}", "idb"], writes=[f"ps{pi}"])
            S.op("pool" if False else "act", lambda: nc.scalar.copy(uT[:, :, t * 128:(t + 1) * 128], pst.rearrange("p (k n) -> p k n", k=8)),
                 reads=[f"ps{pi}"], writes=[f"uT{t // 4}"])

        wf = [ph.sb(f"wf{i}", [128, 8, 512], F32) for i in range(2)]
        wb = [ph.sb(f"wb{i}", [128, 8, 512], BF16) for i in range(2)]
        stg = [ph.sb(f"stg{i}", [128, L], F32) for i in range(2)]
        stb = [ph.sb(f"stb{i}", [128, L], BF16) for i in range(2)]
        vst = [ph.sb(f"vst{i}", [128, 512], BF16) for i in range(4)]
        if even:
            cw = ph.sb("cw", [128, 3, 24], F32)
            cb = ph.sb("cb", [128, 24], F32)
            for kk in range(3):
                src = bass.AP(P.t_conv_w, j * 3 * 3072 + kk * 3072, [[1, 128], [128, 24]])
                S.dma("sp", cw[:, kk, :], src, writes=["cw"], allow_slow_non_contiguous=True)
            src = bass.AP(P.t_conv_b, j * 3072, [[1, 128], [128, 24]])
            S.dma("sp", cb[:], src, writes=["cb"], allow_slow_non_contiguous=True)
        w_in = P.t_w_in.ap()[layer].rearrange("(k p) n -> p k n", p=128)
        n_st = 0
        n_vs = 0
        for blk in range(16):
            c0 = blk * 512
            wi = blk % 2
            S.dma("sp", wf[wi][:], w_in[:, :, c0:c0 + 512], writes=[f"wf{wi}"])
            S.op("pool", lambda: nc.gpsimd.tensor_copy(wb[wi][:], wf[wi][:]), reads=[f"wf{wi}"], writes=[f"wb{wi}"])
            if even:
                kind = "conv" if c0 < 3072 else "gate" if c0 < 4096 else "q" if c0 < 5120 else "k" if c0 < 6144 else "v" if c0 < 7168 else "gate"
            else:
                kind = "q" if c0 < 2048 else "k" if c0 < 4096 else "v" if c0 < 6144 else "gate"
            if kind == "v":
                vc0 = c0 - (6144 if even else 4096)
                for tt in range(NT):
                    pi = P.next_ps()
                    for k in range(8):
                        S.op("pe", lambda: nc.tensor.matmul(P.ps[pi][:], uT[:, k, tt * 128:(tt + 1) * 128], wb[wi][:, k, :], start=(k == 0), stop=(k == 7)),
                             reads=[f"uT{tt // 4}", f"wb{wi}"], writes=[f"ps{pi}"])
                    vb = n_vs % 4
                    n_vs += 1
                    if tt % 2 == 0:
                        S.op("act", lambda: nc.scalar.copy(vst[vb][:], P.ps[pi][:]), reads=[f"ps{pi}"], writes=[f"vst{vb}"])
                    else:
                        S.op("dve", lambda: nc.vector.tensor_copy(vst[vb][:], P.ps[pi][:]), reads=[f"ps{pi}"], writes=[f"vst{vb}"])
                    S.dma("pool", P.t_VTM.ap()[tt * 128:(tt + 1) * 128, vc0:vc0 + 512], vst[vb][:], reads=[f"vst{vb}"], writes=["VTM"])
                continue
            for ct in range(4):
                ch0 = c0 + ct * 128
                sb_i = n_st % 2
                n_st += 1
                for tg in range(8):
                    pi = P.next_ps()
                    for k in range(8):
                        S.op("pe", lambda: nc.tensor.matmul(P.ps[pi][:], wb[wi][:, k, ct * 128:(ct + 1) * 128], uT[:, k, tg * 512:(tg + 1) * 512], start=(k == 0), stop=(k == 7)),
                             reads=[f"uT{tg}", f"wb{wi}"], writes=[f"ps{pi}"])
                    sl = slice(tg * 512, (tg + 1) * 512)
                    if kind == "conv":
                        if tg % 2 == 0:
                            S.op("act", lambda: nc.scalar.copy(stg[sb_i][:, sl], P.ps[pi][:]), reads=[f"ps{pi}"], writes=[f"stg{sb_i}"])
                        else:
                            S.op("dve", lambda: nc.vector.tensor_copy(stg[sb_i][:, sl], P.ps[pi][:]), reads=[f"ps{pi}"], writes=[f"stg{sb_i}"])
                    elif kind == "gate":
                        S.op("act", lambda: nc.scalar.activation(stb[sb_i][:, sl], P.ps[pi][:], AF.Silu), reads=[f"ps{pi}"], writes=[f"stb{sb_i}"])
                    elif kind == "q":
                        S.op("act", lambda: nc.scalar.mul(stb[sb_i][:, sl], P.ps[pi][:], 0.125), reads=[f"ps{pi}"], writes=[f"stb{sb_i}"])
                    else:
                        if tg % 2 == 0:
                            S.op("act", lambda: nc.scalar.copy(stb[sb_i][:, sl], P.ps[pi][:]), reads=[f"ps{pi}"], writes=[f"stb{sb_i}"])
                        else:
                            S.op("dve", lambda: nc.vector.tensor_copy(stb[sb_i][:, sl], P.ps[pi][:]), reads=[f"ps{pi}"], writes=[f"stb{sb_i}"])
                if kind == "conv":
                    cti = ch0 // 128
                    s = stg[sb_i]
                    o = stb[sb_i]
                    S.op("act", lambda: nc.scalar.activation(o[:], s[:], AF.Identity, bias=cb[:, cti:cti + 1], scale=cw[:, 1, cti:cti + 1]),
                         reads=[f"stg{sb_i}", "cw", "cb"], writes=[f"stb{sb_i}"])
                    S.op("dve", lambda: nc.vector.scalar_tensor_tensor(o[:, 1:L], s[:, 0:L - 1], cw[:, 0, cti:cti + 1], o[:, 1:L], ALU.mult, ALU.add),
                         reads=[f"stg{sb_i}", "cw", f"stb{sb_i}"], writes=[f"stb{sb_i}"])
                    S.op("dve", lambda: nc.vector.scalar_tensor_tensor(o[:, 0:L - 1], s[:, 1:L], cw[:, 2, cti:cti + 1], o[:, 0:L - 1], ALU.mult, ALU.add),
                         reads=[f"stg{sb_i}", "cw", f"stb{sb_i}"], writes=[f"stb{sb_i}"])
                S.dma("pool", P.t_PT.ap()[ch0:ch0 + 128, :], stb[sb_i][:], reads=[f"stb{sb_i}"], writes=["PT"])


def phase_outproj(P, layer, hsrc, hdst):
    nc, S = P.nc, P.S
    with Phase(P) as ph:
        idf, idb = load_ident(P, ph)
        wo = ph.sb("wo", [128, 16, D], BF16)
        pg = ph.sb("pg", [128, 8, D], BF16)
        pp = ph.sb("pp", [128, 2, D], BF16)
        wtmp = [ph.sb(f"wtmp{i}", [128, 4, D], F32) for i in range(2)]
        gpost = ph.sb("gpost", [128, D], F32)
        gple = ph.sb("gple", [128, D], F32)
        S.dma("sp", gpost[:], bcast_row(P.t_norm_post, layer * D, D), writes=["gpost"])
        S.dma("sp", gple[:], bcast_row(P.t_ple_norm, layer * D, D), writes=["gple"])
        w_out = P.t_w_out.ap()[layer].rearrange("(k p) n -> p k n", p=128)
        w_pg = P.t_ple_gate.ap()[layer].rearrange("(k p) n -> p k n", p=128)
        w_pp = P.t_ple_proj.ap()[layer].rearrange("(k p) n -> p k n", p=128)
        nld = 0
        for c in range(4):
            b = nld % 2
            nld += 1
            S.dma("sp", wtmp[b][:], w_out[:, c * 4:(c + 1) * 4, :], writes=[f"wtmp{b}"])
            S.op("pool", lambda: nc.gpsimd.tensor_copy(wo[:, c * 4:(c + 1) * 4, :], wtmp[b][:]), reads=[f"wtmp{b}"], writes=["wo"])
        for c in range(2):
            b = nld % 2
            nld += 1
            S.dma("sp", wtmp[b][:], w_pg[:, c * 4:(c + 1) * 4, :], writes=[f"wtmp{b}"])
            S.op("pool", lambda: nc.gpsimd.tensor_copy(pg[:, c * 4:(c + 1) * 4, :], wtmp[b][:]), reads=[f"wtmp{b}"], writes=["pg"])
        b = nld % 2
        S.dma("sp", wtmp[b][:, 0:2, :], w_pp, writes=[f"wtmp{b}"])
        S.op("pool", lambda: nc.gpsimd.tensor_copy(pp[:], wtmp[b][:, 0:2, :]), reads=[f"wtmp{b}"], writes=["pp"])

        yt = [ph.sb(f"yt{i}", [128, 16, 128], BF16) for i in range(2)]
        ht = [ph.sb(f"ht{i}", [128, D], F32) for i in range(2)]
        ptl = [ph.sb(f"ptl{i}", [128, 256], F32) for i in range(2)]
        ptb_l = [ph.sb(f"ptb{i}", [128, 256], BF16) for i in range(2)]
        pT_l = [ph.sb(f"pT{i}", [128, 2, 128], BF16) for i in range(2)]
        sq_l = [ph.sb(f"sq{i}", [128, 512], F32) for i in range(2)]
        ms_l = [ph.sb(f"ms{i}", [128, 4], F32) for i in range(2)]
        rs_l = [ph.sb(f"rs{i}", [128, 2], F32) for i in range(2)]
        tmp_l = [ph.sb(f"tmp{i}", [128, D], F32) for i in range(2)]
        h1_l = [ph.sb(f"h1{i}", [128, D], F32) for i in range(2)]
        h1b_l = [ph.sb(f"h1b{i}", [128, D], BF16) for i in range(2)]
        h1T_l = [ph.sb(f"h1T{i}", [128, 8, 128], BF16) for i in range(2)]
        sg_l = [ph.sb(f"sg{i}", [128, D], F32) for i in range(2)]
        ee_l = [ph.sb(f"ee{i}", [128, D], F32) for i in range(2)]
        ho = [ph.sb(f"ho{i}", [128, D], F32) for i in range(2)]
        YT = P.t_YT.ap().rearrange("(k p) n -> p k n", p=128)
        pin = P.t_p.ap()[layer]
        for t in range(NT):
            b = t % 2
            tsl = slice(t * 128, (t + 1) * 128)
            ptb = ptb_l[b]
            pT = pT_l[b]
            sq = sq_l[b]
            ms = ms_l[b]
            rs = rs_l[b]
            tmp = tmp_l[b]
            h1 = h1_l[b]
            h1b = h1b_l[b]
            h1T = h1T_l[b]
            sg = sg_l[b]
            ee = ee_l[b]
            S.dma("sp", yt[b][:], YT[:, :, tsl], writes=[f"yt{b}"])
            S.dma("sp", ht[b][:], hsrc[tsl, :], writes=[f"ht{b}"])
            S.dma("sp", ptl[b][:], pin[tsl, :], writes=[f"ptl{b}"])
            po = [P.next_ps(), P.next_ps()]
            for half in range(2):
                pi = po[half]
                for k in range(16):
                    S.op("pe", lambda: nc.tensor.matmul(P.ps[pi][:], yt[b][:, k, :], wo[:, k, half * 512:(half + 1) * 512], start=(k == 0), stop=(k == 15)),
                         reads=[f"yt{b}", "wo"], writes=[f"ps{pi}"])
                S.op("act", lambda: nc.scalar.activation(sq[:], P.ps[pi][:], AF.Square, scale=1.0 / 32, accum_out=ms[:, half:half + 1]),
                     reads=[f"ps{pi}"], writes=[f"sq_{b}", f"ms_{b}"])
            S.op("dve", lambda: nc.vector.tensor_tensor(rs[:, 0:1], ms[:, 0:1], ms[:, 1:2], ALU.add), reads=[f"ms_{b}"], writes=[f"rs0_{b}"])
            rms_rstd(P, rs[:, 0:1], f"rs0_{b}")
            for half in range(2):
                pi = po[half]
                hs = slice(half * 512, (half + 1) * 512)
                S.op("dve", lambda: nc.vector.scalar_tensor_tensor(tmp[:, hs], P.ps[pi][:], rs[:, 0:1], gpost[:, hs], ALU.mult, ALU.mult),
                     reads=[f"ps{pi}", f"rs0_{b}", "gpost"], writes=[f"tmp_{b}"])
            S.op("pool", lambda: nc.gpsimd.tensor_tensor(h1[:], tmp[:], ht[b][:], ALU.add), reads=[f"tmp_{b}", f"ht{b}"], writes=[f"h1_{b}"])
            S.op("act", lambda: nc.scalar.copy(h1b[:], h1[:]), reads=[f"h1_{b}"], writes=[f"h1b_{b}"])
            pi = P.next_ps()
            pst = P.ps[pi][:].bitcast(BF16)
            for k in range(8):
                S.op("pe", lambda: nc.tensor.transpose(pst[:, k * 128:(k + 1) * 128], h1b[:, k * 128:(k + 1) * 128], idb[:]),
                     reads=[f"h1b_{b}", "idb"], writes=[f"ps{pi}"])
            S.op("dve", lambda: nc.vector.tensor_copy(h1T[:], pst.rearrange("p (k n) -> p k n", k=8)), reads=[f"ps{pi}"], writes=[f"h1T_{b}"])
            S.op("pool", lambda: nc.gpsimd.tensor_copy(ptb[:], ptl[b][:]), reads=[f"ptl{b}"], writes=[f"ptb_{b}"])
            pi = P.next_ps()
            pst2 = P.ps[pi][:].bitcast(BF16)
            for k in range(2):
                S.op("pe", lambda: nc.tensor.transpose(pst2[:, k * 128:(k + 1) * 128], ptb[:, k * 128:(k + 1) * 128], idb[:]),
                     reads=[f"ptb_{b}", "idb"], writes=[f"ps{pi}"])
            S.op("act", lambda: nc.scalar.copy(pT[:], pst2[:, 0:256].rearrange("p (k n) -> p k n", k=2)), reads=[f"ps{pi}"], writes=[f"pT_{b}"])
            for half in range(2):
                pi = P.next_ps()
                hs = slice(half * 512, (half + 1) * 512)
                for k in range(8):
                    S.op("pe", lambda: nc.tensor.matmul(P.ps[pi][:], h1T[:, k, :], pg[:, k, hs], start=(k == 0), stop=(k == 7)),
                         reads=[f"h1T_{b}", "pg"], writes=[f"ps{pi}"])
                S.op("act", lambda: nc.scalar.activation(sg[:, hs], P.ps[pi][:], AF.Sigmoid), reads=[f"ps{pi}"], writes=[f"sg_{b}"])
            pe_ = [P.next_ps(), P.next_ps()]
            for half in range(2):
                pi = pe_[half]
                hs = slice(half * 512, (half + 1) * 512)
                for k in range(2):
                    S.op("pe", lambda: nc.tensor.matmul(P.ps[pi][:], pT[:, k, :], pp[:, k, hs], start=(k == 0), stop=(k == 1)),
                         reads=[f"pT_{b}", "pp"], writes=[f"ps{pi}"])
                S.op("act", lambda: nc.scalar.activation(sq[:], P.ps[pi][:], AF.Square, scale=1.0 / 32, accum_out=ms[:, 2 + half:3 + half]),
                     reads=[f"ps{pi}"], writes=[f"sq_{b}", f"ms_{b}"])
            S.op("dve", lambda: nc.vector.tensor_tensor(rs[:, 1:2], ms[:, 2:3], ms[:, 3:4], ALU.add), reads=[f"ms_{b}"], writes=[f"rs1_{b}"])
            rms_rstd(P, rs[:, 1:2], f"rs1_{b}")
            for half in range(2):
                pi = pe_[half]
                hs = slice(half * 512, (half + 1) * 512)
                S.op("dve", lambda: nc.vector.scalar_tensor_tensor(ee[:, hs], P.ps[pi][:], rs[:, 1:2], gple[:, hs], ALU.mult, ALU.mult),
                     reads=[f"ps{pi}", f"rs1_{b}", "gple"], writes=[f"ee_{b}"])
            S.op("pool", lambda: nc.gpsimd.tensor_tensor(ee[:], ee[:], sg[:], ALU.mult), reads=[f"ee_{b}", f"sg_{b}"], writes=[f"ee_{b}"])
            S.op("pool", lambda: nc.gpsimd.tensor_tensor(ho[b][:], ee[:], h1[:], ALU.add), reads=[f"ee_{b}", f"h1_{b}"], writes=[f"ho{b}"])
            S.dma("pool", hdst[tsl, :], ho[b][:], reads=[f"ho{b}"], writes=["hdst"])


W_SPECS = [
    ("x", [L, D]), ("p", [DEPTH, L, 256]), ("w_in", [DEPTH, D, 8192]), ("w_out", [DEPTH, 2048, D]),
    ("norm_pre", [DEPTH, D]), ("norm_post", [DEPTH, D]),
    ("hyena_conv_w", [2, 3, 3072]), ("hyena_conv_b", [2, 3072]),
    ("hyena_w1", [2, 33, 64]), ("hyena_b1", [2, 64]), ("hyena_w2", [2, 64, 64]), ("hyena_b2", [2, 64]),
    ("hyena_w3", [2, 64, 64]), ("hyena_b3", [2, 64]), ("hyena_w4", [2, 64, 4096]), ("hyena_freq", [2, 64]),
    ("hyena_skip", [2, 2, 1024]), ("rel_bias", [32, 16]), ("na_rpb", [2, 32, 15, 31]),
    ("ple_proj", [DEPTH, 256, D]), ("ple_norm", [DEPTH, D]), ("ple_gate", [DEPTH, D, D]),
]


def declare_io(P):
    for name, shape in W_SPECS:
        setattr(P, "t_" + name.replace("hyena_", ""), P.din(name, shape))
    P.t_conv_w = P.t_conv_w
    P.c_ident = P.din("c_ident", [128, 128])
    P.t_PT = P.dscratch("PT", [8192, L], BF16)
    P.t_VTM = P.dscratch("VTM", [L, 2048], BF16)
    P.t_YT = P.dscratch("YT", [2048, L], BF16)
    P.t_HB = P.dscratch("HB", [L, D], F32)
    P.t_out = P.nc.dram_tensor("out", [L, D], F32, kind="ExternalOutput")
    P.t_na_rpb_f = P.din("na_rpb_f", [2, 32, 15, 31])
    P.t_GD = P.dscratch("GD", [16, 2304], BF16)
    P.t_GN = P.dscratch("GN", [32, 15 * 128], BF16)
    P.t_SKD = P.dscratch("SKD", [128, 16 * 2304], BF16)
    P.t_SKN = P.dscratch("SKN", [64, 32 * 15 * 128], BF16)
    P.na_tab = na_mask_table()
    P.t_KF = P.dscratch("KF", [2, 8, 128, 33 * 2 * 128], BF16)
    P.t_RN = P.dscratch("RN", [128, 16], F32)
    for name, shape in HY_CONST_SHAPES.items():
        setattr(P, name, P.din(name, shape))
    P.c_na_am = P.din("c_na_am", list(P.na_tab[0].shape))
    P.c_dil_oh = P.din("c_dil_oh", [32, 2303])
    P.c_dil_mult = P.din("c_dil_mult", [16, 2303])


def const_inputs():
    oh, mult = dil_tables()
    d = {"c_ident": np.eye(128, dtype=np.float32), "c_na_am": na_mask_table()[0], "c_dil_oh": oh, "c_dil_mult": mult}
    d.update(hyena_tables())
    return d


NA_CLASSES = [(0, [0, 1, 2, 3]), (1, [-1, 0, 1, 2]), (2, [-2, -1, 0, 1, 2]), (30, [-2, -1, 0, 1]), (31, [-3, -2, -1, 0])]


def na_class_of(qi):
    return 0 if qi == 0 else 1 if qi == 1 else 3 if qi == 30 else 4 if qi == 31 else 2


def na_mask_table():
    tiles, ds, starts = [], [], []
    kc = np.arange(64)
    cs = np.clip(kc - 8, 0, 48)
    colok = (kc[:, None] >= cs[None, :]) & (kc[:, None] < cs[None, :] + 16)
    for qi, dl in NA_CLASSES:
        starts.append(len(tiles))
        for d in dl:
            kt = qi + d
            m = np.zeros((128, 128), np.float32)
            for krl in range(2):
                for qrl in range(2):
                    kr, qr = 2 * kt + krl, 2 * qi + qrl
                    rs = min(max(qr - 4, 0), 56)
                    if rs <= kr < rs + 8:
                        m[krl * 64:(krl + 1) * 64, qrl * 64:(qrl + 1) * 64] = colok
            tiles.append(m)
            ds.append(d)
    AM = np.stack(tiles, axis=1)
    return np.ascontiguousarray(AM), ds, starts


def t5_bucket_np(rel):
    rel = np.asarray(rel, np.int64)
    ret = np.where(rel > 0, 16, 0)
    n = np.abs(rel)
    nf = np.maximum(n, 1).astype(np.float32)
    large = 8 + (np.log(nf / np.float32(8)) / np.float32(math.log(1024 / 8)) * np.float32(8)).astype(np.int32)
    large = np.minimum(large, 15)
    return ret + np.where(n < 8, n, large)


def dil_tables():
    delta = 1151 - np.arange(2303)
    b = t5_bucket_np(delta)
    OH = np.zeros((32, 2303), np.float32)
    OH[b, np.arange(2303)] = 1.0
    a = np.abs(delta)
    mult = (a <= 64).astype(np.float32) + ((delta % 4 == 0) & (a <= 256)) + ((delta % 16 == 0) & (a <= 1024))
    MULT = np.tile(mult[None, :].astype(np.float32), (16, 1))
    return OH, MULT


def build_dil_table(P):
    nc, S = P.nc, P.S
    with Phase(P) as ph:
        rb = ph.sb("rb", [32, 16], F32)
        oh = ph.sb("oh", [32, 2304], F32)
        mu = ph.sb("mu", [16, 2304], F32)
        ge = ph.sb("ge", [16, 2304], F32)
        gb = ph.sb("gb", [16, 2304], BF16)
        S.dma("sp", rb[:], P.t_rel_bias.ap(), writes=["rb"])
        S.dma("sp", oh[:, 0:2303], P.c_dil_oh.ap(), writes=["oh"])
        S.dma("sp", mu[:, 0:2303], P.c_dil_mult.ap(), writes=["mu"])
        S.op("pool", lambda: nc.gpsimd.memset(gb[:], 0.0), writes=["gb"])
        for c in range(5):
            w = min(512, 2303 - c * 512)
            sl = slice(c * 512, c * 512 + w)
            pi = P.next_ps()
            S.op("pe", lambda: nc.tensor.matmul(P.ps[pi][0:16, 0:w], rb[:], oh[:, sl], start=True, stop=True), reads=["rb", "oh"], writes=[f"ps{pi}"])
            S.op("act", lambda: nc.scalar.activation(ge[:, sl], P.ps[pi][0:16, 0:w], AF.Exp), reads=[f"ps{pi}"], writes=["ge"])
            S.op("dve", lambda: nc.vector.tensor_tensor(gb[:, sl], ge[:, sl], mu[:, sl], ALU.mult), reads=["ge", "mu", "gb"], writes=["gb"])
        S.dma("sp", P.t_GD.ap(), gb[:], reads=["gb"], writes=["GD"])
        S.dma("sp", P.t_SKD.ap(), bass.AP(P.t_GD, 0, [[0, 128], [1, 16 * 2304]]), reads=["GD"], writes=["SKD"])


def build_na_table(P, j):
    nc, S = P.nc, P.S
    with Phase(P) as ph:
        rp = ph.sb("rp", [32, 15, 31], F32)
        gn = ph.sb("gn", [32, 15, 128], BF16)
        S.dma("sp", rp[:], P.t_na_rpb_f.ap()[j], writes=["rp"])
        S.op("pool", lambda: nc.gpsimd.memset(gn[:], 0.0), writes=["gn"])
        S.op("act", lambda: nc.scalar.activation(gn[:, :, 48:79], rp[:], AF.Exp), reads=["rp", "gn"], writes=["gn"])
        S.dma("sp", P.t_GN.ap().rearrange("h (r c) -> h r c", c=128), gn[:], reads=["gn"], writes=["GN"])
        S.dma("sp", P.t_SKN.ap(), bass.AP(P.t_GN, 0, [[0, 64], [1, 32 * 15 * 128]]), reads=["GN"], writes=["SKN"])


def phase_attention(P, layer):
    nc, S = P.nc, P.S
    even = layer % 2 == 0
    if even:
        npairs, q0, k0, g0, v0, y0 = 8, 4096, 5120, 7168, 0, 1024
        ND = 17
    else:
        npairs, q0, k0, g0, v0, y0 = 16, 0, 2048, 6144, 0, 0
        ND = 7
        AM_np, na_ds, na_starts = P.na_tab
        NM = AM_np.shape[1]
    PT = P.t_PT.ap()
    VTM = P.t_VTM.ap()
    with Phase(P) as ph:
        idf, idb = load_ident(P, ph)
        QT = [ph.sb(f"QT{i}", [128, L], BF16) for i in range(2)]
        KT = [ph.sb(f"KT{i}", [128, L], BF16) for i in range(2)]
        GT = [ph.sb(f"GT{i}", [128, L], BF16) for i in range(2)]
        V2 = [ph.sb(f"V2{i}", [128, NT, 130], BF16) for i in range(2)]
        YS = [ph.sb(f"YS{i}", [128, L], BF16) for i in range(2)]
        for i in range(2):
            S.op("pool", lambda: nc.gpsimd.memset(V2[i][:], 1.0), writes=[f"V2{i}"])
        EH = [ph.sb(f"EH{i}", [128, ND, 128], BF16) for i in range(4)]
        if not even:
            amf = ph.sb("amf", [128, NM, 128], F32)
            am = ph.sb("am", [128, NM, 128], BF16)
            S.dma("sp", amf[:], P.c_na_am.ap(), writes=["amf"])
            S.op("pool", lambda: nc.gpsimd.tensor_copy(am[:], amf[:]), reads=["amf"], writes=["am"])
            EF = [ph.sb(f"EF{i}", [128, NM, 128], BF16) for i in range(4)]
        PX = [ph.sb(f"PX{i}", [128, 17 * 128], BF16) for i in range(2)]
        PM = [ph.sb(f"PM{i}", [128, 17 * 128], BF16) for i in range(3)]
        OS = [ph.sb(f"OS{i}", [128, 128], BF16) for i in range(2)]
        rd = [ph.sb(f"rd{i}", [128, 2], F32) for i in range(2)]
        npx = 0
        for hp in range(npairs):
            pb = hp % 2
            ch = hp * 128
            S.dma("sp", QT[pb][:], PT[q0 + ch:q0 + ch + 128, :], writes=[f"QT{pb}"])
            S.dma("sp", KT[pb][:], PT[k0 + ch:k0 + ch + 128, :], writes=[f"KT{pb}"])
            S.dma("sp", GT[pb][:], PT[g0 + ch:g0 + ch + 128, :], writes=[f"GT{pb}"])
            for h in range(2):
                src = VTM[:, v0 + ch + h * 64:v0 + ch + h * 64 + 64].rearrange("(t p) c -> p t c", p=128)
                S.dma("sp", V2[pb][:, :, h * 65:h * 65 + 64], src, writes=[f"V2{pb}"])
            for h in range(2):
                hd = hp * 2 + h
                ei = pb * 2 + h
                if even:
                    src = bass.AP(P.t_SKD, hd * 2304 + 127, [[16 * 2304 - 1, 128], [128, 17], [1, 128]])
                    S.dma("sp", EH[ei][:], src, writes=[f"EH{ei}"])
                else:
                    for krl in range(2):
                        for qrl in range(2):
                            off = (hd * 15 + (-6 + krl - qrl + 7)) * 128 + 63
                            src = bass.AP(P.t_SKN, off, [[32 * 15 * 128 - 1, 64], [2 * 128, 7], [1, 64]])
                            S.dma("sp", EH[ei][krl * 64:(krl + 1) * 64, :, qrl * 64:(qrl + 1) * 64], src, writes=[f"EH{ei}"])
                    for m in range(NM):
                        d = na_ds[m]
                        S.op("pool", lambda: nc.gpsimd.tensor_tensor(EF[ei][:, m, :], EH[ei][:, d + 3, :], am[:, m, :], ALU.mult),
                             reads=[f"EH{ei}", "am"], writes=[f"EF{ei}"])
            po_of = {}

            def stage_a(qi, h):
                nonlocal npx
                qsl = slice(qi * 128, (qi + 1) * 128)
                ei = pb * 2 + h
                hs = slice(h * 64, (h + 1) * 64)
                if even:
                    kts = list(range(min(31, qi + 8), max(0, qi - 8) - 1, -1))
                    mview = EH[ei][:, 8 - (kts[0] - qi):8 - (kts[-1] - qi) + 1, :]
                    mname = f"EH{ei}"
                else:
                    cl = na_class_of(qi)
                    dl = NA_CLASSES[cl][1]
                    kts = [qi + d for d in dl]
                    mview = EF[ei][:, na_starts[cl]:na_starts[cl] + len(dl), :]
                    mname = f"EF{ei}"
                n = len(kts)
                xb = npx % 2
                mb = npx % 3
                npx += 1
                for g in range(0, n, 4):
                    gk = kts[g:g + 4]
                    pi = P.rot("att_s", [0, 1, 2, 3, 4])
                    for i, kt in enumerate(gk):
                        S.op("pe", lambda: nc.tensor.matmul(P.ps[pi][:, i * 128:(i + 1) * 128], KT[pb][hs, kt * 128:(kt + 1) * 128], QT[pb][hs, qsl], start=True, stop=True),
                             reads=[f"KT{pb}", f"QT{pb}"], writes=[f"ps{pi}"])
                    S.op("act", lambda: nc.scalar.activation(PX[xb][:, g * 128:(g + len(gk)) * 128], P.ps[pi][:, 0:len(gk) * 128], AF.Exp),
                         reads=[f"ps{pi}"], writes=[f"PX{xb}"])
                S.op("dve", lambda: nc.vector.tensor_tensor(PM[mb][:, 0:n * 128].rearrange("p (m q) -> p m q", q=128), PX[xb][:, 0:n * 128].rearrange("p (m q) -> p m q", q=128), mview, ALU.mult),
                     reads=[f"PX{xb}", mname], writes=[f"PM{mb}"])
                return kts, mb

            def stage_b(qi, h, kts, mb):
                qsl = slice(qi * 128, (qi + 1) * 128)
                if h == 0:
                    po_of[qi] = P.rot("att_o", [5, 6])
                po = po_of[qi]
                n = len(kts)
                for i, kt in enumerate(kts):
                    S.op("pe", lambda: nc.tensor.matmul(P.ps[po][:, h * 65:(h + 1) * 65], PM[mb][:, i * 128:(i + 1) * 128], V2[pb][:, kt, h * 65:(h + 1) * 65], start=(i == 0), stop=(i == n - 1)),
                         reads=[f"PM{mb}", f"V2{pb}"], writes=[f"ps{po}"])
                if h == 0:
                    return
                ob = qi % 2
                S.op("dve", lambda: nc.vector.reciprocal(rd[ob][:], P.ps[po][:, 64:130:65]), reads=[f"ps{po}"], writes=[f"rd{ob}"])
                rdb = rd[ob][:]
                rdb = bass.AP(rdb.tensor, rdb.offset, list(rdb.ap) + [[0, 64]])
                S.op("dve", lambda: nc.vector.tensor_tensor(OS[ob][:].rearrange("p (h d) -> p h d", d=64), P.ps[po][:, 0:130].rearrange("p (h d) -> p h d", d=65)[:, :, 0:64], rdb, ALU.mult),
                     reads=[f"ps{po}", f"rd{ob}"], writes=[f"OS{ob}"])
                pt_ = 7
                ptv = P.ps[pt_][:].bitcast(BF16)[:, (qi % 4) * 128:(qi % 4 + 1) * 128]
                S.op("pe", lambda: nc.tensor.transpose(ptv[:, 0:128], OS[ob][:], idb[:]), reads=[f"OS{ob}", "idb"], writes=[f"ps{pt_}"])
                S.op("dve", lambda: nc.vector.tensor_tensor(YS[pb][:, qsl], ptv[:, 0:128], GT[pb][:, qsl], ALU.mult),
                     reads=[f"ps{pt_}", f"GT{pb}"], writes=[f"YS{pb}"])

            pending = None
            for qi in range(NT):
                for h in range(2):
                    info = stage_a(qi, h)
                    if pending is not None:
                        stage_b(*pending)
                    pending = (qi, h) + info
            stage_b(*pending)
            S.dma("pool", P.t_YT.ap()[y0 + ch:y0 + ch + 128, :], YS[pb][:], reads=[f"YS{pb}"], writes=["YT"])


I32 = mybir.dt.int32
NFFT = 8192


def hyena_tables():
    t = {}
    n2 = np.arange(64)[:, None].astype(np.float64)
    jj = np.arange(64)[None, :]
    t["c_f64"] = np.where(jj <= 32, np.cos(2 * np.pi * n2 * jj / 64), -np.sin(2 * np.pi * n2 * (jj - 32) / 64)).astype(np.float32)
    n1 = np.arange(128)[:, None, None]
    k2 = np.arange(33)[None, :, None]
    k1 = np.arange(128)[None, None, :]
    phi = 2 * np.pi * ((n1 * (64 * k1 + k2)) % NFFT) / NFFT
    t["c_mr"] = np.cos(phi).astype(np.float32).reshape(128, 33 * 128)
    t["c_mi"] = (-np.sin(phi)).astype(np.float32).reshape(128, 33 * 128)
    k1_ = np.arange(128)[:, None]
    n1_ = np.arange(128)[None, :]
    th = 2 * np.pi * ((n1_ * k1_) % 128) / 128
    t["c_cos"] = np.cos(th).astype(np.float32)
    t["c_sin"] = np.sin(th).astype(np.float32)
    k2 = np.arange(33)[:, None, None]
    n1 = np.arange(128)[None, :, None]
    n2 = np.arange(32)[None, None, :]
    w = np.where((k2 == 0) | (k2 == 32), 1.0, 2.0)
    psi = 2 * np.pi * (((n1 + 128 * n2) * k2) % NFFT) / NFFT
    nr = (w * np.cos(psi) / NFFT)
    ni = (-w * np.sin(psi) / NFFT)
    t["c_nri"] = np.stack([nr, ni], axis=1).astype(np.float32).reshape(66, 128 * 32)
    tt = np.linspace(0.0, 1.0, L)
    fr = np.linspace(1e-4, 15.0, 16)[:, None]
    wpos = (2.0 * np.pi / L) * np.arange(L)[None, :]
    z = np.concatenate([tt[None, :], np.cos(fr * wpos), -np.sin(fr * wpos)], axis=0)
    t["c_z"] = z.astype(np.float32)
    z2 = z.copy()
    z2[:, 1:] = z[:, :0:-1]
    t["c_z2"] = z2.astype(np.float32)
    mind = math.log(1e-2) / 1.5
    maxd = math.log(1e-2) / 0.3
    deltas = np.abs(np.linspace(mind, maxd, 1024))
    t["c_negdelta"] = np.ascontiguousarray((-deltas).reshape(8, 128).T).astype(np.float32)
    return t


HY_CONST_SHAPES = {"c_f64": [64, 64], "c_mr": [128, 33 * 128], "c_mi": [128, 33 * 128], "c_cos": [128, 128], "c_sin": [128, 128],
                   "c_nri": [66, 4096], "c_z": [33, L], "c_z2": [33, L], "c_negdelta": [128, 8]}


def hy_load_consts(P, ph, stage, stage_name, inverse):
    nc, S = P.nc, P.S
    C = {}

    def ld(name, parts, n, neg=False):
        t = ph.sb("k" + name, [parts, n], BF16)
        src = getattr(P, name).ap()
        for c0 in range(0, n, 2048):
            w = min(2048, n - c0)
            S.dma("sp", stage[0:parts, 0:w], src[:, c0:c0 + w], writes=[stage_name])
            S.op("pool", lambda: nc.gpsimd.tensor_copy(t[:, c0:c0 + w], stage[0:parts, 0:w]), reads=[stage_name], writes=["k" + name])
        return t

    C["f64"] = ld("c_f64", 64, 64)
    C["mr"] = ld("c_mr", 128, 33 * 128)
    C["mi"] = ld("c_mi", 128, 33 * 128)
    mn = ph.sb("kc_min", [128, 33 * 128], BF16)
    S.op("pool", lambda: nc.gpsimd.tensor_scalar(mn[:], C["mi"][:], -1.0, None, ALU.mult), reads=["kc_mi"], writes=["kc_min"])
    C["min"] = mn
    if inverse:
        C["cos"] = ld("c_cos", 128, 128)
        C["sin"] = ld("c_sin", 128, 128)
        ns = ph.sb("kc_nsin", [128, 128], BF16)
        S.op("pool", lambda: nc.gpsimd.tensor_scalar(ns[:], C["sin"][:], -1.0, None, ALU.mult), reads=["kc_sin"], writes=["kc_nsin"])
        C["nsin"] = ns
        C["nri"] = ld("c_nri", 66, 4096)
    return C


def hy_fwd(P, C, idb, sT, sT_name, XC, A_sb, A_name, Xout, Xout_name, nblk=32):
    nc, S = P.nc, P.S
    Xb = XC[0:nblk, :].rearrange("p (n c) -> p n c", c=128)
    nev = 0
    for g in range(16):
        pi = P.rot("hy_a", [0, 1, 2, 3])
        pv = P.ps[pi][:].bitcast(BF16)
        for i in range(8):
            n1 = g * 8 + i
            S.op("pe", lambda: nc.tensor.transpose(pv[0:nblk, i * 128:(i + 1) * 128], sT[:, n1:nblk * 128:128], idb[:]),
                 reads=[sT_name, "idb"], writes=[f"ps{pi}"])
        dst = XC[0:nblk, g * 1024:(g + 1) * 1024]
        if g % 2 == 0:
            S.op("act", lambda: nc.scalar.copy(dst, pv[0:nblk, :]), reads=[f"ps{pi}"], writes=["XC"])
        else:
            S.op("dve", lambda: nc.vector.tensor_copy(dst, pv[0:nblk, :]), reads=[f"ps{pi}"], writes=["XC"])
    for g in range(16):
        pi = P.rot("hy_a", [0, 1, 2, 3])
        for i in range(8):
            c = g * 8 + i
            S.op("pe", lambda: nc.tensor.matmul(P.ps[pi][:, i * 64:(i + 1) * 64], Xb[:, :, c], C["f64"][0:nblk, :], start=True, stop=True),
                 reads=["XC", "kc_f64"], writes=[f"ps{pi}"])
        dst = A_sb[:, :, g * 8:(g + 1) * 8]
        src = P.ps[pi][:].rearrange("p (c j) -> p j c", j=64)
        if g % 2 == 0:
            S.op("act", lambda: nc.scalar.copy(dst, src), reads=[f"ps{pi}"], writes=[A_name])
        else:
            S.op("dve", lambda: nc.vector.tensor_copy(dst, src), reads=[f"ps{pi}"], writes=[A_name])
    mr = C["mr"][:].rearrange("p (k m) -> p k m", m=128)
    mi = C["mi"][:].rearrange("p (k m) -> p k m", m=128)
    mn = C["min"][:].rearrange("p (k m) -> p k m", m=128)
    for kp in range(17):
        pi = P.rot("hy_b", [4, 5, 6, 7])
        ks = [k for k in (2 * kp, 2 * kp + 1) if k <= 32]
        for i, k2 in enumerate(ks):
            xr = P.ps[pi][:, (2 * i) * 128:(2 * i + 1) * 128]
            xi = P.ps[pi][:, (2 * i + 1) * 128:(2 * i + 2) * 128]
            ar = A_sb[:, k2, :]
            rd = [A_name, "kc_mr", "kc_mi", "kc_min"]
            if k2 in (0, 32):
                S.op("pe", lambda: nc.tensor.matmul(xr, mr[:, k2, :], ar, start=True, stop=True), reads=rd, writes=[f"ps{pi}"])
                S.op("pe", lambda: nc.tensor.matmul(xi, mi[:, k2, :], ar, start=True, stop=True), reads=rd, writes=[f"ps{pi}"])
            else:
                ai = A_sb[:, 32 + k2, :]
                S.op("pe", lambda: nc.tensor.matmul(xr, mr[:, k2, :], ar, start=True, stop=False), reads=rd, writes=[f"ps{pi}"])
                S.op("pe", lambda: nc.tensor.matmul(xr, mn[:, k2, :], ai, start=False, stop=True), reads=rd, writes=[f"ps{pi}"])
                S.op("pe", lambda: nc.tensor.matmul(xi, mi[:, k2, :], ar, start=True, stop=False), reads=rd, writes=[f"ps{pi}"])
                S.op("pe", lambda: nc.tensor.matmul(xi, mr[:, k2, :], ai, start=False, stop=True), reads=rd, writes=[f"ps{pi}"])
        nk = len(ks)
        dst = Xout[:, ks[0]:ks[0] + nk, :, :]
        src = P.ps[pi][:, 0:nk * 256].rearrange("p (k r c) -> p k r c", r=2, c=128)
        if kp % 2 == 0:
            S.op("act", lambda: nc.scalar.copy(dst, src), reads=[f"ps{pi}"], writes=[Xout_name])
        else:
            S.op("dve", lambda: nc.vector.tensor_copy(dst, src), reads=[f"ps{pi}"], writes=[Xout_name])


def hy_inv(P, C, idb, Yv, Y_name, Bs, B_name, XC, yT, yT_name):
    nc, S = P.nc, P.S
    Cv = XC[0:66, :].rearrange("p (c x) -> p c x", x=128)
    nri = C["nri"][:].rearrange("p (a b) -> p a b", b=32)
    yv = yT.rearrange("p (b a) -> p a b", a=128)
    Bf = Bs[:].rearrange("p k r c -> p (k r) c")
    for g in range(9):
        k0 = g * 4
        nk = min(4, 33 - k0)
        yr = Yv[:, k0:k0 + nk, 0, :]
        yi = Yv[:, k0:k0 + nk, 1, :]
        pr = P.rot("hy_b", [4, 5, 6, 7])
        orr = P.ps[pr][:, 0:nk * 128].rearrange("p (k c) -> p k c", c=128)
        S.op("pe", lambda: nc.tensor.matmul(orr, C["cos"][:], yr, start=True, stop=False), reads=[Y_name, "kc_cos"], writes=[f"ps{pr}"])
        S.op("pe", lambda: nc.tensor.matmul(orr, C["nsin"][:], yi, start=False, stop=True), reads=[Y_name, "kc_nsin"], writes=[f"ps{pr}"])
        pi_ = P.rot("hy_b", [4, 5, 6, 7])
        oi = P.ps[pi_][:, 0:nk * 128].rearrange("p (k c) -> p k c", c=128)
        S.op("pe", lambda: nc.tensor.matmul(oi, C["sin"][:], yr, start=True, stop=False), reads=[Y_name, "kc_sin"], writes=[f"ps{pi_}"])
        S.op("pe", lambda: nc.tensor.matmul(oi, C["cos"][:], yi, start=False, stop=True), reads=[Y_name, "kc_cos"], writes=[f"ps{pi_}"])
        S.op("act", lambda: nc.scalar.copy(Bs[:, k0:k0 + nk, 0, :], orr), reads=[f"ps{pr}"], writes=[B_name])
        S.op("dve", lambda: nc.vector.tensor_copy(Bs[:, k0:k0 + nk, 1, :], oi), reads=[f"ps{pi_}"], writes=[B_name])
    for g in range(16):
        pi = P.rot("hy_a", [0, 1, 2, 3])
        pv = P.ps[pi][:].bitcast(BF16)
        for i in range(8):
            c = g * 8 + i
            S.op("pe", lambda: nc.tensor.transpose(pv[0:66, i * 128:(i + 1) * 128], Bf[:, :, c], idb[:]), reads=[B_name, "idb"], writes=[f"ps{pi}"])
        dst = XC[0:66, g * 1024:(g + 1) * 1024]
        if g % 2 == 0:
            S.op("act", lambda: nc.scalar.copy(dst, pv[0:66, :]), reads=[f"ps{pi}"], writes=["XC"])
        else:
            S.op("dve", lambda: nc.vector.tensor_copy(dst, pv[0:66, :]), reads=[f"ps{pi}"], writes=["XC"])
    for g in range(8):
        pi = P.rot("hy_b", [4, 5, 6, 7])
        for i in range(16):
            n1 = g * 16 + i
            o = P.ps[pi][:, i * 32:(i + 1) * 32]
            S.op("pe", lambda: nc.tensor.matmul(o, Cv[:, :, n1], nri[:, n1, :], start=True, stop=True), reads=["XC", "kc_nri"], writes=[f"ps{pi}"])
        dst = yv[:, g * 16:(g + 1) * 16, :]
        src = P.ps[pi][:].rearrange("p (a b) -> p a b", b=32)
        if g % 2 == 0:
            S.op("act", lambda: nc.scalar.copy(dst, src), reads=[f"ps{pi}"], writes=[yT_name])
        else:
            S.op("dve", lambda: nc.vector.tensor_copy(dst, src), reads=[f"ps{pi}"], writes=[yT_name])


def phase_hyena_filters(P, layer):
    nc, S = P.nc, P.S
    j = layer // 2
    TWO_PI = 2.0 * math.pi
    with Phase(P) as ph:
        idf, idb = load_ident(P, ph)
        B = [ph.sb(f"B{i}", [128, L], F32) for i in range(6)]
        C = hy_load_consts(P, ph, B[0], "B0", inverse=False)
        XC = ph.sb("XC", [128, 16384], BF16)
        A_sb = ph.sb("A_sb", [128, 64, 128], BF16)
        X0 = ph.sb("X0", [128, 33, 2, 128], BF16)
        w1 = ph.sb("w1", [33, 64], F32)
        w2 = ph.sb("w2", [64, 64], F32)
        w3 = ph.sb("w3", [64, 64], F32)
        bq = ph.sb("bq", [64, 8], F32)
        w4s = [ph.sb(f"w4s{i}", [64, 128], F32) for i in range(2)]
        w4b = [ph.sb(f"w4b{i}", [64, 128], BF16) for i in range(2)]
        nd = ph.sb("nd", [128, 8], F32)
        sm = ph.sb("sm", [128, 2, 16], F32)
        S.dma("sp", w1[:], P.t_w1.ap()[j], writes=["w1"])
        S.dma("sp", w2[:], P.t_w2.ap()[j], writes=["w2"])
        S.dma("sp", w3[:], P.t_w3.ap()[j], writes=["w3"])
        for i, t in enumerate([P.t_b1, P.t_b2, P.t_b3, P.t_freq]):
            S.dma("sp", bq[:, i:i + 1], bass.AP(t, j * 64, [[1, 64], [1, 1]]), writes=["bq"])
        S.dma("sp", nd[:], P.c_negdelta.ap(), writes=["nd"])
        S.op("dve", lambda: nc.vector.tensor_scalar(bq[:, 4:5], bq[:, 3:4], 1.0 / TWO_PI, None, ALU.mult), reads=["bq"], writes=["bq"])
        for i in range(3):
            S.op("dve", lambda: nc.vector.tensor_scalar(bq[:, 5 + i:6 + i], bq[:, i:i + 1], bq[:, 4:5], 8.5, ALU.mult, ALU.add), reads=["bq"], writes=["bq"])

        def mlp(inp, iname, K, w, wname, bi, out, oname):
            it = B[2][0:64, :].bitcast(I32)
            mt = B[3][0:64, :]
            for c in range(8):
                sl = slice(c * 512, (c + 1) * 512)
                pi = P.rot("hy_a", [0, 1, 2, 3])
                S.op("pe", lambda: nc.tensor.matmul(P.ps[pi][0:64, :], w[0:K, :], inp[0:K, sl], start=True, stop=True), reads=[iname, wname], writes=[f"ps{pi}"])
                S.op("dve", lambda: nc.vector.tensor_scalar(out[0:64, sl], P.ps[pi][0:64, :], bq[:, 4:5], bq[:, 5 + bi:6 + bi], ALU.mult, ALU.add),
                     reads=[f"ps{pi}", "bq"], writes=[oname])
            o = out[0:64, :]
            S.op("dve", lambda: nc.vector.tensor_copy(it, o), reads=[oname], writes=["B2"])
            S.op("dve", lambda: nc.vector.tensor_tensor(o, o, it, ALU.subtract), reads=[oname, "B2"], writes=[oname])
            S.op("dve", lambda: nc.vector.tensor_scalar(mt, o, 0.0, None, ALU.is_lt), reads=[oname], writes=["B3"])
            S.op("pool", lambda: nc.gpsimd.tensor_tensor(o, o, mt, ALU.add), reads=[oname, "B3"], writes=[oname])
            S.op("act", lambda: nc.scalar.activation(o, o, AF.Sin, bias=-math.pi, scale=TWO_PI), reads=[oname], writes=[oname])

        for zsrc, a3i in ((P.c_z, 4), (P.c_z2, 5)):
            S.dma("sp", B[0][0:33, :], zsrc.ap(), writes=["B0"])
            mlp(B[0], "B0", 33, w1, "w1", 0, B[1], "B1")
            mlp(B[1], "B1", 64, w2, "w2", 1, B[0], "B0")
            mlp(B[0], "B0", 64, w3, "w3", 2, B[1], "B1")
            S.op("dve", lambda: nc.vector.tensor_copy(B[a3i][:].bitcast(BF16)[0:64, 0:L], B[1][0:64, :]), reads=["B1"], writes=[f"B{a3i}"])
        S.dma("sp", B[0][:], bcast_row(P.c_z, 0, L), writes=["B0"])
        S.dma("sp", B[2][:], bcast_row(P.c_z2, 0, L), writes=["B2"])
        kc = B[3][:].bitcast(BF16)
        junk = A_sb[:].rearrange("p j c -> p (j c)")
        w4 = P.t_w4.ap()[j]
        for ct in range(8):
            for f in range(2):
                for dr in range(2):
                    tl, tln = (B[0], "B0") if dr == 0 else (B[2], "B2")
                    a3x, a3n = (B[4], "B4") if dr == 0 else (B[5], "B5")
                    S.op("act", lambda: nc.scalar.activation(B[1][:], tl[:], AF.Exp, scale=nd[:, ct:ct + 1]), reads=[tln, "nd"], writes=["B1"])
                    ctp = f * 16 + dr * 8 + ct
                    S.dma("sp", w4s[dr][:], w4[:, ctp * 128:(ctp + 1) * 128], writes=[f"w4s{dr}"])
                    S.op("pool", lambda: nc.gpsimd.tensor_copy(w4b[dr][:], w4s[dr][:]), reads=[f"w4s{dr}"], writes=[f"w4b{dr}"])
                    a3v = a3x[:].bitcast(BF16)
                    for c in range(8):
                        sl = slice(c * 512, (c + 1) * 512)
                        osl = slice(dr * L + c * 512, dr * L + (c + 1) * 512)
                        pi = P.rot("hy_a", [0, 1, 2, 3])
                        S.op("pe", lambda: nc.tensor.matmul(P.ps[pi][:], w4b[dr][:], a3v[0:64, sl], start=True, stop=True), reads=[a3n, f"w4b{dr}"], writes=[f"ps{pi}"])
                        S.op("dve", lambda: nc.vector.tensor_tensor(kc[:, osl], P.ps[pi][:], B[1][:, sl], ALU.mult), reads=[f"ps{pi}", "B1"], writes=["B3"])
                S.op("pool", lambda: nc.gpsimd.memset(kc[:, L:L + 1], 0.0), writes=["B3"])
                col = f * 8 + ct
                S.op("act", lambda: nc.scalar.activation(junk, kc, AF.Abs, accum_out=sm[:, 0, col:col + 1]), reads=["B3"], writes=["A_sb", "sm"])
                S.op("dve", lambda: nc.vector.reciprocal(sm[:, 1, col:col + 1], sm[:, 0, col:col + 1]), reads=["sm"], writes=["sm"])
                hy_fwd(P, C, idb, kc, "B3", XC, A_sb, "A_sb", X0, "X0", nblk=64)
                S.dma("sp", P.t_KF.ap()[f, ct], X0[:].rearrange("p k r c -> p (k r c)"), reads=["X0"], writes=["KF"])
        S.dma("sp", P.t_RN.ap(), sm[:, 1, :], reads=["sm"], writes=["RN"])


def phase_hyena_conv(P, layer):
    nc, S = P.nc, P.S
    j = layer // 2
    PT = P.t_PT.ap()
    with Phase(P) as ph:
        idf, idb = load_ident(P, ph)
        yT = ph.sb("yT", [128, L], F32)
        C = hy_load_consts(P, ph, yT, "yT", inverse=True)
        XC = ph.sb("XC", [128, 16384], BF16)
        AY = ph.sb("AY", [128, 33 * 2 * 128], BF16)
        A_sb = AY[:, 0:8192].rearrange("p (j c) -> p j c", c=128)
        Yv = AY[:].rearrange("p (k r c) -> p k r c", r=2, c=128)
        Xs = ph.sb("Xs", [128, 33, 2, 128], BF16)
        Kf = ph.sb("Kf", [128, 33, 2, 128], BF16)
        T1 = ph.sb("T1", [128, 33, 128], BF16)
        T2 = ph.sb("T2", [128, 33, 128], BF16)
        vT = ph.sb("vT", [128, L], BF16)
        x1T = ph.sb("x1T", [128, L], BF16)
        zT = ph.sb("zT", [128, L], BF16)
        ost = ph.sb("ost", [128, L], BF16)
        sk = ph.sb("sk", [128, 2, 8], F32)
        rn = ph.sb("rn", [128, 16], F32)
        S.dma("sp", rn[:], P.t_RN.ap(), writes=["rn"])
        for f in range(2):
            S.dma("sp", sk[:, f, :], bass.AP(P.t_skip, (j * 2 + f) * 1024, [[1, 128], [128, 8]]), writes=["sk"], allow_slow_non_contiguous=True)

        def conv(f, ct, sT, sname):
            S.dma("sp", Kf[:].rearrange("p k r c -> p (k r c)"), P.t_KF.ap()[f, ct], writes=["Kf"])
            hy_fwd(P, C, idb, sT, sname, XC, A_sb, "AY", Xs, "Xs")
            xr, xi = Xs[:, :, 0, :], Xs[:, :, 1, :]
            kr, ki = Kf[:, :, 0, :], Kf[:, :, 1, :]
            S.op("dve", lambda: nc.vector.tensor_tensor(T1[:], xr, kr, ALU.mult), reads=["Xs", "Kf"], writes=["T1"])
            S.op("pool", lambda: nc.gpsimd.tensor_tensor(T2[:], xi, ki, ALU.mult), reads=["Xs", "Kf"], writes=["T2"])
            S.op("dve", lambda: nc.vector.tensor_tensor(Yv[:, :, 0, :], T1[:], T2[:], ALU.subtract), reads=["T1", "T2"], writes=["AY"])
            S.op("pool", lambda: nc.gpsimd.tensor_tensor(T1[:], xr, ki, ALU.mult), reads=["Xs", "Kf"], writes=["T1"])
            S.op("dve", lambda: nc.vector.tensor_tensor(T2[:], xi, kr, ALU.mult), reads=["Xs", "Kf"], writes=["T2"])
            S.op("pool", lambda: nc.gpsimd.tensor_tensor(Yv[:, :, 1, :], T1[:], T2[:], ALU.add), reads=["T1", "T2"], writes=["AY"])
            hy_inv(P, C, idb, Yv, "AY", Xs, "Xs", XC, yT[:], "yT")
            S.op("act", lambda: nc.scalar.activation(yT[:], yT[:], AF.Copy, scale=rn[:, f * 8 + ct:f * 8 + ct + 1]), reads=["yT", "rn"], writes=["yT"])

        for ct in range(8):
            r0 = ct * 128
            S.dma("sp", vT[:], PT[r0:r0 + 128, :], writes=["vT"])
            S.dma("sp", x1T[:], PT[1024 + r0:1024 + r0 + 128, :], writes=["x1T"])
            conv(0, ct, vT[:], "vT")
            S.op("dve", lambda: nc.vector.scalar_tensor_tensor(yT[:], vT[:], sk[:, 0, ct:ct + 1], yT[:], ALU.mult, ALU.add), reads=["vT", "sk", "yT"], writes=["yT"])
            S.op("pool", lambda: nc.gpsimd.tensor_tensor(zT[:], yT[:], x1T[:], ALU.mult), reads=["yT", "x1T"], writes=["zT"])
            S.dma("sp", vT[:], PT[2048 + r0:2048 + r0 + 128, :], writes=["vT"])
            S.dma("sp", x1T[:], PT[3072 + r0:3072 + r0 + 128, :], writes=["x1T"])
            conv(1, ct, zT[:], "zT")
            S.op("dve", lambda: nc.vector.scalar_tensor_tensor(yT[:], zT[:], sk[:, 1, ct:ct + 1], yT[:], ALU.mult, ALU.add), reads=["zT", "sk", "yT"], writes=["yT"])
            S.op("pool", lambda: nc.gpsimd.tensor_tensor(yT[:], yT[:], vT[:], ALU.mult), reads=["yT", "vT"], writes=["yT"])
            S.op("dve", lambda: nc.vector.tensor_tensor(ost[:], yT[:], x1T[:], ALU.mult), reads=["yT", "x1T"], writes=["ost"])
            S.dma("pool", P.t_YT.ap()[r0:r0 + 128, :], ost[:], reads=["ost"], writes=["YT"])


def phase_hyena(P, layer):
    phase_hyena_filters(P, layer)
    phase_hyena_conv(P, layer)


def build_program(dbg=None, layers=range(DEPTH)):
    P = Prog(dbg=dbg)
    declare_io(P)
    P.t_HB2 = P.dscratch("HB2", [L, D], F32)
    hb = [P.t_HB.ap(), P.t_HB2.ap()]
    layers = list(layers)
    built_dil = False
    for li, layer in enumerate(layers):
        hsrc = P.t_x.ap() if li == 0 else hb[(li - 1) % 2]
        hdst = P.t_out.ap() if li == len(layers) - 1 else hb[li % 2]
        phase_inproj(P, layer, hsrc)
        if layer % 2 == 0:
            if not built_dil:
                build_dil_table(P)
                built_dil = True
            phase_attention(P, layer)
            phase_hyena(P, layer)
        else:
            build_na_table(P, layer // 2)
            phase_attention(P, layer)
        phase_outproj(P, layer, hsrc, hdst)
    P.S.wait_all("sp")
    return P


_CACHE = {}


def kernel(**inputs):
    if "P" not in _CACHE:
        _CACHE["P"] = build_program()
        _CACHE["consts"] = const_inputs()
    P = _CACHE["P"]
    consts = _CACHE["consts"]
    B = inputs["x"].shape[0]
    shared = {}
    for name, shape in W_SPECS:
        if name in ("x", "p"):
            continue
        shared[name] = np.ascontiguousarray(inputs[name], dtype=np.float32)
    shared["na_rpb_f"] = np.ascontiguousarray(np.asarray(inputs["na_rpb"], dtype=np.float32)[..., ::-1])
    shared.update(consts)
    in_maps = []
    for b in range(B):
        m = dict(shared)
        m["x"] = np.ascontiguousarray(inputs["x"][b], dtype=np.float32)
        m["p"] = np.ascontiguousarray(np.asarray(inputs["p"])[:, b], dtype=np.float32)
        in_maps.append({k: v for k, v in m.items() if k in P.inputs})
    res = run_bass_kernel_spmd(P.nc, in_maps, core_ids=list(range(B)))
    out = np.stack([np.asarray(r["out"], dtype=np.float32) for r in res.results], axis=0)
    return out
```
